# Optimizing a Trainium2 kernel written in Bass

```python
import jax, jax.numpy as jnp
from jax import lax
import numpy as np

D_MODEL = 1024
BATCH = 8
SEQ = 4096
DEPTH = 1
DEC_BATCH = 32
DEC_SEQ = 16
PAST_LEN = 1024

CHUNK = 64
N_MEM = 256
MEM_HEADS = 4
MEM_HEAD_DIM = D_MODEL // MEM_HEADS
D_POOL = D_MODEL // 2
POOL_WINDOWS = (2, 4, 8, 16)
POOL_GROUPS = len(POOL_WINDOWS)
POOL_GW = D_POOL // POOL_GROUPS
POOL_STATE = max(POOL_WINDOWS) - 1
D_SGU = D_MODEL // 2
SGU_HEADS = 4
SGU_HW = D_SGU // SGU_HEADS
SGU_CHUNK = 128
D_FF = 2816
D_IN = D_POOL + 2 * D_SGU + 2 * D_MODEL
EPS = 1e-6

kernel_name = "hybrid_pool_sgu_streaming_step"


def _rmsnorm(x, g):
    xf = x.astype(jnp.float32)
    y = xf * lax.rsqrt(jnp.mean(xf * xf, axis=-1, keepdims=True) + EPS)
    return (y * g.astype(jnp.float32)).astype(x.dtype)


def _layernorm(x, g):
    xf = x.astype(jnp.float32)
    mu = jnp.mean(xf, axis=-1, keepdims=True)
    xc = xf - mu
    y = xc * lax.rsqrt(jnp.mean(xc * xc, axis=-1, keepdims=True) + EPS)
    return (y * g.astype(jnp.float32)).astype(x.dtype)


def _swiglu(x, w1, w3, w2):
    return (jax.nn.silu(x @ w1) * (x @ w3)) @ w2


def _sgu_mask():
    blk = jnp.arange(SGU_CHUNK) // CHUNK
    return blk[:, None] >= blk[None, :]


def _pool_mixer(xa, prefix, pos0, w_pool, pool_scale):
    B, L, _ = xa.shape
    cat = jnp.concatenate([prefix.astype(xa.dtype), xa], axis=1)
    csum = jnp.cumsum(cat.astype(jnp.float32), axis=1)
    csum = jnp.concatenate([jnp.zeros_like(csum[:, :1]), csum], axis=1)
    end = csum[:, POOL_STATE + 1:]
    pos = pos0 + jnp.arange(L)
    outs = []
    for g, w in enumerate(POOL_WINDOWS):
        sl = slice(g * POOL_GW, (g + 1) * POOL_GW)
        start = csum[:, POOL_STATE + 1 - w: POOL_STATE + 1 - w + L, sl]
        cnt = jnp.minimum(w, pos + 1).astype(jnp.float32)[None, :, None]
        outs.append((end[..., sl] - start) / cnt)
    pooled = jnp.concatenate(outs, axis=-1).astype(xa.dtype)
    d = (pooled - xa).reshape(B, L, POOL_GROUPS, POOL_GW)
    mixed = jnp.einsum('blgc,gcd->blgd', d, w_pool).reshape(B, L, D_POOL)
    return mixed * pool_scale, cat[:, -POOL_STATE:]


def _sgu_prompt(v, w_s, b_s):
    B, L, _ = v.shape
    vb = v.reshape(B, L // SGU_CHUNK, SGU_CHUNK, SGU_HEADS, SGU_HW)
    wm = jnp.where(_sgu_mask()[None], w_s, 0.0).astype(v.dtype)
    z = jnp.einsum('hpq,bnqhc->bnphc', wm, vb) + b_s.T[:, :, None]
    return z.reshape(B, L, D_SGU)


def _sgu_sample(v, w_s, b_s):
    B, T, _ = v.shape
    vb = v.reshape(B, T, SGU_HEADS, SGU_HW)
    wm = jnp.where(_sgu_mask()[None], w_s, 0.0).astype(v.dtype)[:, :T, :T]
    z = jnp.einsum('hpq,bqhc->bphc', wm, vb) + b_s[:, :T].T[:, :, None]
    return z.reshape(B, T, D_SGU)


def _mem_kv(mem, g_mem, w_mk, w_mv):
    B = mem.shape[0]
    mn = _rmsnorm(mem, g_mem)
    k = (mn @ w_mk).reshape(B, N_MEM, MEM_HEADS, MEM_HEAD_DIM)
    v = (mn @ w_mv).reshape(B, N_MEM, MEM_HEADS, MEM_HEAD_DIM)
    return k, v


def _layer(x, mem_k, mem_v, pool_prefix, pos0, is_prompt, p):
    B, L, _ = x.shape
    h = x + 0.5 * _swiglu(_rmsnorm(x, p['g_ff1']), p['w1a'], p['w3a'], p['w2a'])
    n = _rmsnorm(h, p['g_mix'])
    z = n @ p['w_in']
    xa = z[..., :D_POOL]
    uv = jax.nn.gelu(z[..., D_POOL:D_POOL + 2 * D_SGU])
    u, v = uv[..., :D_SGU], uv[..., D_SGU:]
    gate = jax.nn.sigmoid(z[..., D_POOL + 2 * D_SGU:] + p['b_gate'])
    g_a, g_b = gate[..., :D_MODEL], gate[..., D_MODEL:]
    a, pool_state = _pool_mixer(xa, pool_prefix, pos0, p['w_pool'], p['pool_scale'])
    vn = _layernorm(v, p['g_sgu'])
    s = _sgu_prompt(vn, p['w_s'], p['b_s']) if is_prompt else _sgu_sample(vn, p['w_s'], p['b_s'])
    merged = g_a * (a @ p['w_pa']) + g_b * ((u * s) @ p['w_pb'])
    h = h + merged @ p['w_o']
    q = (_rmsnorm(h, p['g_ca']) @ p['w_q']).reshape(B, L, MEM_HEADS, MEM_HEAD_DIM)
    sc = jnp.einsum('blhd,bmhd->bhlm', q, mem_k).astype(jnp.float32) * (MEM_HEAD_DIM ** -0.5)
    pr = jax.nn.softmax(sc, axis=-1).astype(x.dtype)
    o = jnp.einsum('bhlm,bmhd->blhd', pr, mem_v).reshape(B, L, D_MODEL)
    h = h + o @ p['w_co']
    h = h + 0.5 * _swiglu(_rmsnorm(h, p['g_ff2']), p['w1b'], p['w3b'], p['w2b'])
    return h, pool_state, vn


def setup_inputs(seed: int = 0) -> dict:
    ks = iter(jax.random.split(jax.random.key(seed), 64))
    nrm = lambda shape, scale: jax.random.normal(next(ks), shape, jnp.float32) * scale
    gain = lambda shape: 1.0 + 0.1 * jax.random.normal(next(ks), shape, jnp.float32)
    Lr = DEPTH
    return {
        "x_prompt": nrm((BATCH, SEQ, D_MODEL), 1.0),
        "x_sample": nrm((DEC_BATCH, DEC_SEQ, D_MODEL), 1.0),
        "state_pool": nrm((Lr, DEC_BATCH, POOL_STATE, D_POOL), 1.0),
        "cache_mem_k": nrm((Lr, DEC_BATCH, N_MEM, MEM_HEADS, MEM_HEAD_DIM), 1.0),
        "cache_mem_v": nrm((Lr, DEC_BATCH, N_MEM, MEM_HEADS, MEM_HEAD_DIM), 1.0),
        "mem_prompt": nrm((BATCH, N_MEM, D_MODEL), 1.0),
        "g_ff1": gain((Lr, D_MODEL)),
        "w1a": nrm((Lr, D_MODEL, D_FF), D_MODEL ** -0.5),
        "w3a": nrm((Lr, D_MODEL, D_FF), D_MODEL ** -0.5),
        "w2a": nrm((Lr, D_FF, D_MODEL), D_FF ** -0.5),
        "g_mix": gain((Lr, D_MODEL)),
        "w_in": nrm((Lr, D_MODEL, D_IN), D_MODEL ** -0.5),
        "b_gate": nrm((Lr, 2 * D_MODEL), 0.1),
        "w_pool": nrm((Lr, POOL_GROUPS, POOL_GW, POOL_GW), POOL_GW ** -0.5),
        "pool_scale": gain((Lr, D_POOL)),
        "g_sgu": gain((Lr, D_SGU)),
        "w_s": nrm((Lr, SGU_HEADS, SGU_CHUNK, SGU_CHUNK), SGU_CHUNK ** -0.5),
        "b_s": gain((Lr, SGU_HEADS, SGU_CHUNK)),
        "w_pa": nrm((Lr, D_POOL, D_MODEL), D_POOL ** -0.5),
        "w_pb": nrm((Lr, D_SGU, D_MODEL), D_SGU ** -0.5),
        "w_o": nrm((Lr, D_MODEL, D_MODEL), D_MODEL ** -0.5),
        "g_mem": gain((Lr, D_MODEL)),
        "w_mk": nrm((Lr, D_MODEL, D_MODEL), D_MODEL ** -0.5),
        "w_mv": nrm((Lr, D_MODEL, D_MODEL), D_MODEL ** -0.5),
        "g_ca": gain((Lr, D_MODEL)),
        "w_q": nrm((Lr, D_MODEL, D_MODEL), D_MODEL ** -0.5),
        "w_co": nrm((Lr, D_MODEL, D_MODEL), D_MODEL ** -0.5),
        "g_ff2": gain((Lr, D_MODEL)),
        "w1b": nrm((Lr, D_MODEL, D_FF), D_MODEL ** -0.5),
        "w3b": nrm((Lr, D_MODEL, D_FF), D_MODEL ** -0.5),
        "w2b": nrm((Lr, D_FF, D_MODEL), D_FF ** -0.5),
        "g_final": gain((D_MODEL,)),
    }


def reference(x_prompt, x_sample, state_pool, cache_mem_k, cache_mem_v, mem_prompt,
              g_ff1, w1a, w3a, w2a, g_mix, w_in, b_gate, w_pool, pool_scale, g_sgu, w_s, b_s,
              w_pa, w_pb, w_o, g_mem, w_mk, w_mv, g_ca, w_q, w_co, g_ff2, w1b, w3b, w2b, g_final):
    hp, hs = x_prompt, x_sample
    zero_prefix = jnp.zeros((x_prompt.shape[0], POOL_STATE, D_POOL), x_prompt.dtype)
    pool_p_list, pool_s_list, sgu_v_list, mk_list, mv_list = [], [], [], [], []
    for l in range(DEPTH):
        p = dict(g_ff1=g_ff1[l], w1a=w1a[l], w3a=w3a[l], w2a=w2a[l], g_mix=g_mix[l], w_in=w_in[l],
                 b_gate=b_gate[l], w_pool=w_pool[l], pool_scale=pool_scale[l], g_sgu=g_sgu[l],
                 w_s=w_s[l], b_s=b_s[l], w_pa=w_pa[l], w_pb=w_pb[l], w_o=w_o[l], g_ca=g_ca[l],
                 w_q=w_q[l], w_co=w_co[l], g_ff2=g_ff2[l], w1b=w1b[l], w3b=w3b[l], w2b=w2b[l])
        mk_p, mv_p = _mem_kv(mem_prompt, g_mem[l], w_mk[l], w_mv[l])
        hp, pool_p, _ = _layer(hp, mk_p, mv_p, zero_prefix, 0, True, p)
        hs, pool_s, sgu_v = _layer(hs, cache_mem_k[l], cache_mem_v[l], state_pool[l], PAST_LEN, False, p)
        pool_p_list.append(pool_p)
        pool_s_list.append(pool_s)
        sgu_v_list.append(sgu_v)
        mk_list.append(mk_p)
        mv_list.append(mv_p)
    y_prompt = _rmsnorm(hp, g_final)
    y_sample = _rmsnorm(hs, g_final)
    pool_prompt = jnp.stack(pool_p_list)
    pool_sample = jnp.stack(pool_s_list)
    sgu_v_sample = jnp.stack(sgu_v_list)
    mem_k_prompt = jnp.stack(mk_list)
    mem_v_prompt = jnp.stack(mv_list)
    return (y_prompt, y_sample, pool_prompt, pool_sample, sgu_v_sample, mem_k_prompt, mem_v_prompt)
```

```python
import itertools
from contextlib import ExitStack

import numpy as np
import concourse.bass as bass
import concourse.mybir as mybir
from concourse.bass_utils import run_bass_kernel_spmd

F32 = mybir.dt.float32
BF16 = mybir.dt.bfloat16
AF = mybir.ActivationFunctionType
ALU = mybir.AluOpType
AX = mybir.AxisListType

ENGS = ("pe", "act", "dve", "pool", "sp")

D = 1024
SEQ = 4096
NB = 8
DEC_B = 32
DEC_T = 16
SB_PER_CORE = 4
N_MEM = 256
D_POOL = 512
D_SGU = 512
D_FF = 2816
NFC = 22
D_IN = 3584
EPS = 1e-6
NSLOT = 5
TT = 512
N_PTILES = SEQ // TT
WINS = (2, 4, 8, 16)


def _prod(xs):
    r = 1
    for x in xs:
        r *= x
    return r


class Acc:
    __slots__ = ("ap", "buf", "segs")

    def __init__(self, ap, buf, segs):
        self.ap = ap
        self.buf = buf
        self.segs = segs


class View:
    def __init__(self, ap, buf, off, shape, esz):
        self.ap = ap
        self.buf = buf
        self.off = off
        self.shape = tuple(shape)
        self.esz = esz

    def __call__(self, *idx):
        if len(idx) == 0:
            idx = (None,)
        p = idx[0] if idx[0] is not None else slice(None)
        fidx = [slice(None) if ix is None else ix for ix in idx[1:]]
        fidx = fidx + [slice(None)] * (len(self.shape) - len(fidx))
        rng = []
        for d, ix in enumerate(fidx):
            if isinstance(ix, int):
                assert 0 <= ix < self.shape[d], (self.buf, self.shape, idx)
                rng.append((ix, ix + 1))
            else:
                lo = 0 if ix.start is None else ix.start
                hi = self.shape[d] if ix.stop is None else ix.stop
                assert ix.step is None
                assert 0 <= lo < hi <= self.shape[d], (self.buf, self.shape, idx)
                rng.append((lo, hi))
        ap = self.ap[(p,) + tuple(fidx)]
        if isinstance(self.buf, tuple) and self.buf[0] == "psum":
            return Acc(ap, self.buf, [(0, 2048)])
        n = len(self.shape)
        strides = [_prod(self.shape[i + 1:]) for i in range(n)]
        k = n - 1
        while k > 0 and rng[k] == (0, self.shape[k]):
            k -= 1
        outer = [range(lo, hi) for (lo, hi) in rng[:k]]
        nruns = _prod([len(r) for r in outer]) if outer else 1
        segs = []
        if nruns > 64:
            lo = sum(rng[d][0] * strides[d] for d in range(n))
            hi = sum((rng[d][1] - 1) * strides[d] for d in range(n)) + 1
            segs.append((self.off + lo * self.esz, self.off + hi * self.esz))
        else:
            for combo in itertools.product(*outer):
                base = sum(c * strides[d] for d, c in enumerate(combo))
                lo = base + rng[k][0] * strides[k]
                hi = base + rng[k][1] * strides[k]
                segs.append((self.off + lo * self.esz, self.off + hi * self.esz))
        return Acc(ap, self.buf, segs)


class Op:
    __slots__ = ("eng", "fn", "deps", "signal", "cnt", "key", "dval", "pos", "dwaits")

    def __init__(self, eng, fn, key):
        self.pos = 0
        self.dwaits = {}
        self.eng = eng
        self.fn = fn
        self.deps = set()
        self.signal = False
        self.cnt = None
        self.key = key
        self.dval = None


class Tracker:
    def __init__(self):
        self.ops = {e: [] for e in ENGS}
        self.recs = {}
        self.dma_cnt = {}
        self.fence_deps = {e: None for e in ENGS}
        self.all_dma = []

    def _collect(self, op, acc, is_write):
        recs = self.recs.get(acc.buf)
        if not recs:
            return
        for (lo, hi) in acc.segs:
            for r in recs:
                if r[0] < hi and lo < r[1]:
                    d = r[3]
                    if d is op:
                        continue
                    rk = r[2]
                    if not is_write and rk == "R":
                        continue
                    if d.key is None and op.key is None and d.eng == op.eng:
                        if op.eng == "pe":
                            continue
                        if rk == "R" and op.eng != "pool":
                            continue
                    op.deps.add(d)

    def _record(self, op, acc, is_write):
        recs = self.recs.setdefault(acc.buf, [])
        for (lo, hi) in acc.segs:
            if is_write:
                recs[:] = [r for r in recs if not (lo <= r[0] and r[1] <= hi)]
                recs.append([lo, hi, "W", op])
            else:
                for r in recs:
                    if (r[2] == "R" and r[0] == lo and r[1] == hi and r[3].eng == op.eng
                            and (r[3].key is None) == (op.key is None)):
                        r[3] = op
                        break
                else:
                    recs.append([lo, hi, "R", op])

    def add(self, eng, fn, reads=(), writes=(), key=None, ndma=1):
        op = Op(eng, fn, key)
        fd = self.fence_deps[eng]
        if fd:
            op.deps.update(fd)
            self.fence_deps[eng] = None
        for a in reads:
            ex = isinstance(a.buf, tuple) and a.buf[0] == "psum"
            self._collect(op, a, ex)
        for a in writes:
            self._collect(op, a, True)
        for a in reads:
            ex = isinstance(a.buf, tuple) and a.buf[0] == "psum"
            self._record(op, a, ex)
        for a in writes:
            self._record(op, a, True)
        best = {}
        for d in op.deps:
            if d.key is None:
                b = best.get(d.eng)
                if b is None or d.pos > b.pos:
                    best[d.eng] = d
            else:
                op.dwaits[d.key] = self.dma_cnt[d.key]
        op.deps = set(best.values())
        if key is not None:
            self.dma_cnt[key] = self.dma_cnt.get(key, 0) + 16 * ndma
            op.dval = self.dma_cnt[key]
            self.all_dma.append(op)
        for d in op.deps:
            d.signal = True
        op.pos = len(self.ops[eng])
        self.ops[eng].append(op)
        return op

    def fence(self):
        deps = []
        for e in ENGS:
            comp = [o for o in self.ops[e] if o.key is None]
            if comp:
                comp[-1].signal = True
                deps.append(comp[-1])
        deps.extend(self.all_dma)
        for e in ENGS:
            self.fence_deps[e] = list(deps)

    def emit(self, nc, es):
        keys = sorted(self.dma_cnt.keys(), key=str)
        sem_eng = {e: es.enter_context(nc.semaphore("s_" + e)) for e in ENGS}
        sem_dma = {k: es.enter_context(nc.semaphore("d_%d" % i)) for i, k in enumerate(keys)}
        for e in ENGS:
            c = 0
            for op in self.ops[e]:
                if op.key is None and op.signal:
                    c += 1
                    op.cnt = c
        handles = {"pe": nc.tensor, "act": nc.scalar, "dve": nc.vector, "pool": nc.gpsimd, "sp": nc.sync}
        block = es.enter_context(nc.Block())
        tr = self

        def run(e):
            h = handles[e]
            waited = {}
            for op in tr.ops[e]:
                waits = {}
                for d in op.deps:
                    waits[("e", d.eng)] = d.cnt
                for dk, v in op.dwaits.items():
                    waits[("d", dk)] = v
                for k, v in waits.items():
                    if waited.get(k, 0) >= v:
                        continue
                    waited[k] = v
                    h.wait_ge(sem_dma[k[1]] if k[0] == "d" else sem_eng[k[1]], v)
                if op.key is not None:
                    op.fn(sem_dma[op.key])
                else:
                    ins = op.fn()
                    if op.signal:
                        ins.then_inc(sem_eng[e], 1)
            if e == "sp":
                for k in keys:
                    h.wait_ge(sem_dma[k], tr.dma_cnt[k])

        @block.tensor
        def _(t):
            run("pe")

        @block.scalar
        def _(t):
            run("act")

        @block.vector
        def _(t):
            run("dve")

        @block.gpsimd
        def _(t):
            run("pool")

        @block.sync
        def _(t):
            run("sp")


class TileD:
    def __init__(self, kind, idx, NT, S, Pn, R, h=None):
        self.h = h
        self.kind = kind
        self.idx = idx
        self.NT = NT
        self.S = S
        self.Pn = Pn
        self.R = R


WEIGHT_NAMES = ["g_ff1", "w1a", "w3a", "w2a", "g_mix", "w_in", "b_gate", "w_pool", "pool_scale", "g_sgu",
                "w_s", "b_s", "w_pa", "w_pb", "w_o", "g_mem", "w_mk", "w_mv", "g_ca", "w_q", "w_co",
                "g_ff2", "w1b", "w3b", "w2b", "g_final"]
WEIGHT_SHAPES = {
    "g_ff1": [D], "w1a": [D, D_FF], "w3a": [D, D_FF], "w2a": [D_FF, D], "g_mix": [D], "w_in": [D, D_IN],
    "b_gate": [2 * D], "w_pool": [4, 128, 128], "pool_scale": [D_POOL], "g_sgu": [D_SGU], "w_s": [4, 128, 128],
    "b_s": [4, 128], "w_pa": [D_POOL, D], "w_pb": [D_SGU, D], "w_o": [D, D], "g_mem": [D], "w_mk": [D, D],
    "w_mv": [D, D], "g_ca": [D], "w_q": [D, D], "w_co": [D, D], "g_ff2": [D], "w1b": [D, D_FF],
    "w3b": [D, D_FF], "w2b": [D_FF, D], "g_final": [D],
}


def build_program(n_ptiles=N_PTILES, do_sample=True, dumps=()):
    nc = bass.Bass("TRN2", target_bir_lowering=False)
    tr = Tracker()
    dumps = set(dumps)

    def din(name, shape):
        return nc.dram_tensor(name, shape, F32, kind="ExternalInput").ap()

    def dout(name, shape):
        return nc.dram_tensor(name, shape, F32, kind="ExternalOutput").ap()

    xp = din("xp", [SEQ, D])
    xs = din("xs", [SB_PER_CORE * DEC_T, D])
    spool = din("spool", [SB_PER_CORE * 15, D_POOL])
    ck = din("ck", [SB_PER_CORE, N_MEM, D])
    cv = din("cv", [SB_PER_CORE, N_MEM, D])
    mem = din("mem", [N_MEM, D])
    W = {n: din(n, WEIGHT_SHAPES[n]) for n in WEIGHT_NAMES}

    yp = dout("yp", [SEQ, D])
    ys = dout("ys", [SB_PER_CORE * DEC_T, D])
    pool_p = dout("pool_p", [15, D_POOL])
    pool_s = dout("pool_s", [SB_PER_CORE, 15, D_POOL])
    sgu_v = dout("sgu_v", [SB_PER_CORE * DEC_T, D_SGU])
    mk_o = dout("mk_o", [N_MEM, D])
    mv_o = dout("mv_o", [N_MEM, D])
    dump_outs = {}

    fams = {"ffu_a": 11, "ffd_a": 6, "win": 7, "wpab": 2, "wo": 2, "wq": 2, "wco": 2, "wmk": 2, "wmv": 2,
            "ffu_b": 11, "ffd_b": 6}
    scr = {f: nc.dram_tensor("scr_" + f, [n, 128, 4096], BF16, kind="Internal").ap() for f, n in fams.items()}

    with ExitStack() as es:
        E = es.enter_context

        def sb_alloc(name, shape, dt):
            esz = 4 if dt == F32 else 2
            t = E(nc.sbuf_tensor(name, [128, _prod(shape)], dt))
            ap = t[:, :]
            if len(shape) == 2:
                ap = ap.rearrange("p (a b) -> p a b", a=shape[0])
            elif len(shape) == 3:
                ap = ap.rearrange("p (a b c) -> p a b c", a=shape[0], b=shape[1])
            return View(ap, name, 0, shape, esz)

        hbufs = [sb_alloc("h0", [4, D], F32), sb_alloc("h1", [4, D], F32)]
        xnT = sb_alloc("xnT", [8, TT], BF16)
        xntm = sb_alloc("xntm", [4, D], BF16)
        KT = sb_alloc("KT", [8, N_MEM], BF16)
        Vp = sb_alloc("Vp", [2, D], BF16)
        xaT = sb_alloc("xaT", [4, 1, 16 + TT], F32)
        slots_t = [E(nc.sbuf_tensor("slot%d" % i, [128, 4096], BF16)) for i in range(NSLOT)]
        ybuf = sb_alloc("ybuf", [2, 512], F32)
        junk_t = E(nc.sbuf_tensor("junk", [128, D], BF16))
        sx_t = E(nc.sbuf_tensor("sx", [128, 2048], BF16))
        ARENA_B = 75968
        arena_t = E(nc.sbuf_tensor("arena", [128, ARENA_B // 4], F32))
        cst_t = E(nc.sbuf_tensor("cst", [128, 2448], F32))
        cstb_t = E(nc.sbuf_tensor("cstb", [128, 4352], BF16))
        st_t = E(nc.sbuf_tensor("st", [128, 208], F32))
        banks_t = [E(nc.psum_tensor("bank%d" % i, [128, 512], F32)) for i in range(8)]

        def carve(t, bufname, off_b, shape, dt):
            esz = 4 if dt == F32 else 2
            n = _prod(shape)
            tesz = 4 if t.dtype == F32 else 2
            assert off_b % 4 == 0
            ap = t[:, off_b // tesz:(off_b + ((n * esz + 3) // 4) * 4) // tesz]
            if dt != t.dtype:
                ap = ap.bitcast(dt)
            ap = ap[:, 0:n]
            if len(shape) == 2:
                ap = ap.rearrange("p (a b) -> p a b", a=shape[0])
            elif len(shape) == 3:
                ap = ap.rearrange("p (a b c) -> p a b c", a=shape[0], b=shape[1])
            return View(ap, bufname, off_b, shape, esz)

        class Carver:
            def __init__(self, t, name):
                self.t = t
                self.name = name
                self.off = 0

            def __call__(self, shape, dt, at=None):
                if at is not None:
                    self.off = at
                esz = 4 if dt == F32 else 2
                v = carve(self.t, self.name, self.off, shape, dt)
                self.off += ((_prod(shape) * esz + 31) // 32) * 32
                return v

        cc = Carver(cst_t, "cst")
        cfm = cc([64], F32)
        gfin = cc([D], F32)
        gsgu = cc([D_SGU], F32)
        fixv = cc([4, 16], F32)
        neghalf = cc([8], F32)
        identf = cc([128], F32)
        vec_tm = cc([128], F32)
        ws_f = cc([4, 128], F32)
        cb = Carver(cstb_t, "cstb")
        ident = cb([128], BF16)
        wmT = cb([4, 128], BF16)
        wblk = cb([4, 64], BF16)
        wpool = cb([4, 128], BF16)
        ones = cb([128], BF16)
        bsrow = cb([4, 4, 128], BF16)
        bsrow_s = cb([4, 4, 16], BF16)
        ws_b = cb([4, 128], BF16)
        sc = Carver(st_t, "st")
        ss = sc([8, 4], F32)
        tmpn = sc([8, 4], F32)
        rstd = sc([8, 4], F32)
        vsum = sc([4], F32)
        vsq = sc([4], F32)
        lmean = sc([4], F32)
        lmsq = sc([4], F32)
        lvar = sc([4], F32)
        lrstd = sc([4], F32)
        mxv = sc([4, 4], F32)
        negb = sc([4, 4], F32)
        rsv = sc([4, 4], F32)
        rinv = sc([4, 4], F32)

        sxc = Carver(sx_t, "sx")
        xnT_s = sxc([8, 64], BF16)
        h1T_s = sxc([NFC, 64], BF16)
        sil_s = sxc([2, 64], BF16)
        assert sxc.off <= 4096
        ar = Carver(arena_t, "arena")
        h1T = ar([NFC, TT], BF16, at=0)
        sil = ar([2, TT], BF16)
        yfin = ar([2, D], F32, at=14 * TT * 2)
        assert yfin.off + 8192 <= 22 * TT * 2
        uT = ar([4, TT], BF16, at=0)
        vtm = ar([4, D_SGU], F32)
        vnb = ar([4, D_SGU], BF16)
        gates = ar([16, TT], BF16)
        ta = ar([1, 16 + TT], F32)
        tb_ = ar([1, 16 + TT], F32)
        tc_ = ar([1, 16 + TT], F32)
        pooled = ar([4, 1, TT], F32)
        dT = ar([4, TT], BF16)
        aT = ar([4, TT], BF16)
        usT = ar([4, TT], BF16)
        mergedT = ar([8, TT], BF16)
        t1 = ar([2, TT], F32)
        t2 = ar([2, TT], F32)
        assert ar.off <= ARENA_B, ar.off
        qT = ar([8, TT], BF16, at=0)
        Ptm = ar([2, 4, N_MEM], BF16)
        PT = ar([8, TT], BF16)
        oT = ar([8, TT], BF16)
        attn_end = ar.off
        Kst = [ar([2, D], BF16, at=attn_end), ar([2, D], BF16)]
        KTs = [ar([8, N_MEM], BF16), ar([8, N_MEM], BF16)]
        Vs = [ar([2, D], BF16), ar([2, D], BF16)]
        assert ar.off <= ARENA_B, ar.off
        xaT_s = ar([4, 4, 32], F32, at=ta.off)
        ta_s = ar([4, 32], F32)
        tb_s = ar([4, 32], F32)
        tc_s = ar([4, 32], F32)
        pooled_s = ar([4, 4, 16], F32)
        sp_stage = ar([D_POOL], F32)
        assert ar.off <= dT.off, ar.off

        bank_rr = [0]
        bank_pool = [list(range(8))]

        def nextbank(dt=F32, shape=None, fixed=None):
            if fixed is not None:
                i = fixed
            else:
                i = bank_pool[0][bank_rr[0] % len(bank_pool[0])]
                bank_rr[0] += 1
            ap = banks_t[i][:, :]
            esz = 4
            if dt == BF16:
                ap = ap.bitcast(BF16)
                esz = 2
            shape = shape or [2048 // esz]
            ap = ap[:, 0:_prod(shape)]
            if len(shape) == 2:
                ap = ap.rearrange("p (a b) -> p a b", a=shape[0])
            return View(ap, ("psum", i), 0, shape, esz)

        def mm(out, lhsT, rhs, start, stop):
            tr.add("pe", lambda: nc.tensor.matmul(out.ap, lhsT=lhsT.ap, rhs=rhs.ap, start=start, stop=stop),
                   reads=[lhsT, rhs], writes=[out])

        def trp(out, in_, idn):
            tr.add("pe", lambda: nc.tensor.transpose(out.ap, in_.ap, idn.ap), reads=[in_, idn], writes=[out])

        def act(out, in_, func, scale=None, bias=None, accum=None, track_out=True):
            kw = {}
            rd = [in_]
            wr = [out] if track_out else []
            if scale is not None:
                if isinstance(scale, Acc):
                    kw["scale"] = scale.ap
                    rd.append(scale)
                else:
                    kw["scale"] = scale
            if bias is not None:
                if isinstance(bias, Acc):
                    kw["bias"] = bias.ap
                    rd.append(bias)
                else:
                    kw["bias"] = bias
            if accum is not None:
                kw["accum_out"] = accum.ap
                wr.append(accum)
            tr.add("act", lambda: nc.scalar.activation(out=out.ap, in_=in_.ap, func=func, **kw), reads=rd, writes=wr)

        def eh(eng):
            return nc.vector if eng == "dve" else nc.gpsimd

        def tt(eng, out, in0, in1, op, in1_ap=None):
            a1 = in1.ap if in1_ap is None else in1_ap
            tr.add(eng, lambda: eh(eng).tensor_tensor(out=out.ap, in0=in0.ap, in1=a1, op=op),
                   reads=[in0, in1], writes=[out])

        def ts(eng, out, in0, s1, s2, op0, op1=None):
            rd = [in0]
            a1 = s1
            a2 = s2
            if isinstance(s1, Acc):
                rd.append(s1)
                a1 = s1.ap
            if isinstance(s2, Acc):
                rd.append(s2)
                a2 = s2.ap
            if op1 is None:
                tr.add(eng, lambda: eh(eng).tensor_scalar(out=out.ap, in0=in0.ap, scalar1=a1, scalar2=None, op0=op0),
                       reads=rd, writes=[out])
            else:
                tr.add(eng, lambda: eh(eng).tensor_scalar(out=out.ap, in0=in0.ap, scalar1=a1, scalar2=a2, op0=op0, op1=op1),
                       reads=rd, writes=[out])

        def stt(out, in0, scalar, in1, op0, op1):
            rd = [in0, in1]
            a = scalar
            if isinstance(scalar, Acc):
                rd.append(scalar)
                a = scalar.ap
            tr.add("dve", lambda: nc.vector.scalar_tensor_tensor(out=out.ap, in0=in0.ap, scalar=a, in1=in1.ap, op0=op0, op1=op1),
                   reads=rd, writes=[out])

        def cpy(eng, out, in_):
            if eng == "act":
                tr.add("act", lambda: nc.scalar.copy(out=out.ap, in_=in_.ap), reads=[in_], writes=[out])
            else:
                tr.add(eng, lambda: eh(eng).tensor_copy(out=out.ap, in_=in_.ap), reads=[in_], writes=[out])

        def memset(eng, out, val):
            tr.add(eng, lambda: eh(eng).memset(out.ap, val), writes=[out])

        def dma(eng, out_ap, in_ap, reads, writes, key):
            q = {"sp": nc.sync, "pool": nc.gpsimd, "act": nc.scalar}[eng]
            tr.add(eng, lambda s: q.dma_start(out=out_ap, in_=in_ap).then_inc(s, 16), reads=reads, writes=writes, key=key)

        def dump(name, acc, shape):
            if name not in dumps:
                return
            o = dout("dbg_" + name, shape)
            dump_outs[name] = o
            dma("pool", o, acc.ap, [acc], [], ("dbg", name))

        def SL(a, b):
            return slice(a, b)

        def load_x(td):
            if td.kind == "prompt":
                t0 = td.idx * TT
                dma("sp", td.h().ap, xp[t0:t0 + TT, :].rearrange("(s p) d -> p s d", p=128), [], [td.h()], ("xload", td.par))
            elif td.kind == "sample":
                dma("sp", td.h(SL(0, td.Pn), 0).ap, xs[:, :], [], [td.h(None, 0)], ("xload", td.par))
            else:
                dma("sp", td.h(None, SL(0, 2)).ap, mem.rearrange("(s p) d -> p s d", p=128), [], [td.h(None, SL(0, 2))],
                    ("xload", td.par))

        tiles = [TileD("prompt", i, TT, 4, 128, 128) for i in range(n_ptiles)]
        if do_sample:
            tiles.append(TileD("sample", 0, SB_PER_CORE * DEC_T, 1, SB_PER_CORE * DEC_T, DEC_T))
        for i, t in enumerate(tiles):
            t.par = i % 2
            t.h = hbufs[t.par]
        mem_td = TileD("mem", 0, N_MEM, 2, 128, 128, h=hbufs[1])
        mem_td.par = 1
        load_x(mem_td)
        load_x(tiles[0])

        memset("pool", identf(), 0.0)
        tr.add("pool", lambda: nc.gpsimd.affine_select(out=identf().ap, in_=identf().ap, pattern=[[-1, 128]],
                                                       compare_op=ALU.not_equal, fill=1.0, base=0, channel_multiplier=1),
               reads=[identf()], writes=[identf()])
        cpy("dve", ident(), identf())
        memset("pool", neghalf(), -0.5)
        memset("pool", ones(), 1.0)
        memset("pool", xaT(None, SL(0, 4), 0, SL(0, 16)), 0.0)
        memset("pool", wblk(), 0.0)
        for g, w in enumerate(WINS):
            memset("pool", fixv(None, g), 1.0)
            for t in range(w - 1):
                memset("pool", fixv(None, g, SL(t, t + 1)), float(w) / float(t + 1))
        memset("pool", vec_tm(), 0.0)
        rows = [("g_ff1", 0, 8), ("g_mix", 8, 8), ("g_ca", 16, 8), ("g_ff2", 24, 8), ("g_mem", 32, 8),
                ("b_gate", 40, 16), ("pool_scale", 56, 4)]
        for ri, (nm, r0, nr) in enumerate(rows):
            wacc = Acc(None, "cst", [(vec_tm.off + 64 * ri, vec_tm.off + 64 * ri + 64)])
            dma("sp", vec_tm(SL(r0, r0 + nr)).ap, W[nm].rearrange("(r c) -> r c", c=128),
                [], [wacc], "cst_vec")
        dma("sp", gfin().ap, W["g_final"].partition_broadcast(128), [], [gfin()], "cst_g")
        dma("sp", gsgu().ap, W["g_sgu"].partition_broadcast(128), [], [gsgu()], "cst_g")
        dma("sp", ws_f().ap, W["w_s"].rearrange("h p q -> p h q"), [], [ws_f()], "cst_ws")
        for n in range(4):
            dma("pool", bsrow(SL(0, 1), None, n).ap, W["b_s"].rearrange("(o h) p -> o h p", o=1),
                [], [bsrow(SL(0, 1), None, n)], "cst_bs")
            dma("pool", bsrow_s(SL(0, 1), None, n).ap, W["b_s"].rearrange("(o h) p -> o h p", o=1)[:, :, 0:16],
                [], [bsrow_s(SL(0, 1), None, n)], "cst_bss")
        dma("pool", wpool().ap, W["w_pool"].rearrange("g c d -> c g d"), [], [wpool()], "cst_wp")
        bk = nextbank()
        trp(bk(None, SL(0, 64)), vec_tm(SL(0, 64)), identf(SL(0, 64), SL(0, 64)))
        cpy("dve", cfm(), bk(None, SL(0, 64)))
        cpy("dve", ws_b(), ws_f())
        bk = nextbank(BF16, [8, 128])
        for hh in range(4):
            trp(bk(None, hh), ws_b(None, hh), ident())
        cpy("dve", wmT(), bk(None, SL(0, 4)))
        memset("pool", wmT(SL(64, 128), None, SL(0, 64)), 0.0)
        for b in range(4):
            dma("sp", wblk(SL(16 * b, 16 * b + 16), None, SL(16 * b, 16 * b + 16)).ap,
                wmT(SL(0, 16), None, SL(0, 16)).ap, [wmT()], [wblk(None, None, SL(16 * b, 16 * b + 16))], "cst_wblk")


        chunk_parts = {}

        def conv(fam, j, parts):
            chunk_parts[(fam, j)] = parts

        def pump(phase, frac=1.0):
            return

        def kview(w, rows_per=128):
            return w.rearrange("(kc p) n -> p kc n", p=128)

        def conv_ffn(sfx):
            w1 = kview(W["w1" + sfx])
            w3 = kview(W["w3" + sfx])
            w2 = kview(W["w2" + sfx])
            for j in range(11):
                conv("ffu_" + sfx, j, [(0, 8, 256, w1[:, :, j * 256:(j + 1) * 256]),
                                       (2048, 8, 256, w3[:, :, j * 256:(j + 1) * 256])])
            for nh in range(2):
                for c in range(3):
                    f0 = 8 * c
                    nf = min(8, NFC - f0)
                    conv("ffd_" + sfx, nh * 3 + c, [(0, nf, 512, w2[:, f0:f0 + nf, nh * 512:(nh + 1) * 512])])

        def conv_sq(fam, w, js):
            wv = kview(w)
            for j in js:
                conv(fam, j, [(0, 8, 512, wv[:, :, j * 512:(j + 1) * 512])])

        conv_sq("wmk", W["w_mk"], range(2))
        conv_sq("wmv", W["w_mv"], range(2))
        conv_ffn("a")
        wpa = kview(W["w_pa"])
        wpb = kview(W["w_pb"])

        def conv_pab(nh):
            conv("wpab", nh, [(0, 4, 512, wpa[:, :, nh * 512:(nh + 1) * 512]),
                              (2048, 4, 512, wpb[:, :, nh * 512:(nh + 1) * 512])])
        conv_sq("win", W["w_in"], [0, 1, 2, 3, 5])
        conv_pab(0)
        conv_sq("win", W["w_in"], [4, 6])
        conv_pab(1)
        conv_sq("wo", W["w_o"], range(2))
        conv_sq("wq", W["w_q"], range(2))
        conv_sq("wco", W["w_co"], range(2))
        conv_ffn("b")

        slot_rr = [0]

        converted = set()
        first_order = list(chunk_parts.keys())
        cast_emitted = [0]
        CAST_AHEAD = NSLOT - 2

        def emit_casts_upto(q):
            q = min(q, len(first_order) - 1)
            while cast_emitted[0] <= q:
                c = cast_emitted[0]
                i = c % NSLOT
                for (c0, a, b, src) in chunk_parts[first_order[c]]:
                    dst = slots_t[i][:, c0:c0 + a * b].rearrange("p (a b) -> p a b", a=a)
                    dma("pool", dst, src, [], [Acc(None, "slot%d" % i, [(2 * c0, 2 * (c0 + a * b))])], ("slotc", i))
                cast_emitted[0] += 1

        def load_chunk(fam, j, shape):
            cnt = slot_rr[0]
            i = cnt % NSLOT
            slot_rr[0] += 1
            t = slots_t[i]
            n = _prod(shape)
            ap = t[:, 0:n]
            if len(shape) == 2:
                ap = ap.rearrange("p (a b) -> p a b", a=shape[0])
            elif len(shape) == 3:
                ap = ap.rearrange("p (a b c) -> p a b c", a=shape[0], b=shape[1])
            v = View(ap, "slot%d" % i, 0, shape, 2)
            if (fam, j) in converted:
                dma("sp", t[:, 0:n], scr[fam][j][:, 0:n], [Acc(None, ("scr", fam, j), [(0, 2)])], [v()], ("slot", i))
                return v
            assert first_order[cnt] == (fam, j), (first_order[cnt], fam, j)
            converted.add((fam, j))
            emit_casts_upto(cnt + CAST_AHEAD)
            if fam not in ("wmk", "wmv"):
                dma("sp", scr[fam][j][:, 0:n], t[:, 0:n], [v()], [Acc(None, ("scr", fam, j), [(0, 2)])], ("scw", i))
            return v

        def norm_groups(td):
            S = td.S
            return [(0, S)] if S <= 2 else [(0, 2), (2, S)]

        def norm_stats(td, nidx, a, b):
            Pn = td.Pn
            pl = SL(0, Pn)
            for s in range(a, b):
                act(Acc(junk_t[0:Pn, :], "junk", [(0, 2048)]), td.h(pl, s), AF.Square, accum=ss(pl, nidx, SL(s, s + 1)))
            ts("pool", tmpn(pl, nidx, SL(a, b)), ss(pl, nidx, SL(a, b)), 1.0 / D, EPS, ALU.mult, ALU.add)
            tt("pool", rstd(pl, nidx, SL(a, b)), tmpn(pl, nidx, SL(a, b)), neghalf(pl, SL(a, b)), ALU.pow)

        def rmsnorm_T(td, nidx, gcol, parts="AB", dst=None):
            dst = xnT if dst is None else dst
            Pn = td.Pn
            pl = SL(0, Pn)
            groups = norm_groups(td)

            def copy_scaled(s, eng):
                if eng == "act":
                    act(xntm(pl, s), td.h(pl, s), AF.Copy, scale=rstd(pl, nidx, SL(s, s + 1)))
                else:
                    ts("dve", xntm(pl, s), td.h(pl, s), rstd(pl, nidx, SL(s, s + 1)), None, ALU.mult)

            def tr_evac(s):
                bv = nextbank(BF16, [8, 128])
                for kc in range(8):
                    trp(bv(None, kc, SL(0, Pn)), xntm(pl, s, SL(kc * 128, (kc + 1) * 128)), ident(pl, SL(0, Pn)))
                g = cfm(None, SL(gcol, gcol + 8))
                tt("dve", dst(None, None, SL(s * 128, s * 128 + Pn)), bv(None, None, SL(0, Pn)), g, ALU.mult,
                   in1_ap=g.ap.unsqueeze(2).to_broadcast([128, 8, Pn]))

            if "A" in parts:
                for (a, b) in groups:
                    norm_stats(td, nidx, a, b)
            last = len(groups) - 1
            for gi_, (a, b) in enumerate(groups):
                if "A" in parts:
                    for s in range(a, b):
                        copy_scaled(s, "act" if (gi_ == last and s == a) else "dve")
                if "B" in parts:
                    for s in range(a, b):
                        tr_evac(s)

        def ffn(td, sfx, nidx, gcol, prenormed=False, hook_mid=None, hook_up_done=None, extra=None, extra_prenorm=None):
            NT, S, Pn = td.NT, td.S, td.Pn
            pl = SL(0, Pn)
            if not prenormed:
                rmsnorm_T(td, nidx, gcol)
            if sfx == "a":
                pump("mix")
            else:
                pump("ffb")
            for j in range(11):
                sl = load_chunk("ffu_" + sfx, j, [2, 8, 256])
                for half in range(2):
                    fc = 2 * j + half
                    bA = nextbank()
                    bB = nextbank()
                    for (bk_, wi) in ((bA, 0), (bB, 1)):
                        for kc in range(8):
                            mm(bk_(None, SL(0, NT)), sl(None, wi, kc, SL(half * 128, (half + 1) * 128)),
                               xnT(None, kc, SL(0, NT)), kc == 0, kc == 7)
                    act(sil(None, fc % 2, SL(0, NT)), bA(None, SL(0, NT)), AF.Silu)
                    tt("dve", h1T(None, fc, SL(0, NT)), bB(None, SL(0, NT)), sil(None, fc % 2, SL(0, NT)), ALU.mult)
                if extra is not None:
                    if j == 0 and extra_prenorm is not None:
                        extra_prenorm()
                    for half in range(2):
                        fc = 2 * j + half
                        bS = nextbank()
                        for wi in range(2):
                            for kc in range(8):
                                mm(bS(None, SL(wi * 64, wi * 64 + 64)), sl(None, wi, kc, SL(half * 128, (half + 1) * 128)),
                                   xnT_s(None, kc), kc == 0, kc == 7)
                        act(sil_s(None, fc % 2), bS(None, SL(0, 64)), AF.Silu)
                        tt("dve", h1T_s(None, fc), bS(None, SL(64, 128)), sil_s(None, fc % 2), ALU.mult)
                if j == 5 and hook_mid is not None:
                    hook_mid()
            if hook_up_done is not None:
                hook_up_done()
            for nh in range(2):
                bks = [nextbank() for _ in range(S)]
                if extra is not None:
                    bkS = nextbank()
                    plS = SL(0, extra.Pn)
                for c in range(3):
                    f0 = 8 * c
                    nf = min(8, NFC - f0)
                    sl = load_chunk("ffd_" + sfx, nh * 3 + c, [nf, 512])
                    if extra is not None:
                        for fl in range(nf):
                            fc = f0 + fl
                            mm(bkS(plS), h1T_s(None, fc), sl(None, fl), fc == 0, fc == NFC - 1)
                        if c == 2:
                            hsS = extra.h(plS, 0, SL(nh * 512, (nh + 1) * 512))
                            stt(hsS, bkS(plS), 0.5, hsS, ALU.mult, ALU.add)
                    if c < 2:
                        for fl in range(nf):
                            fc = f0 + fl
                            for s in range(S):
                                mm(bks[s](pl), h1T(None, fc, SL(s * 128, s * 128 + Pn)), sl(None, fl), fc == 0, False)
                    else:
                        for s in range(S):
                            for fl in range(nf):
                                fc = f0 + fl
                                mm(bks[s](pl), h1T(None, fc, SL(s * 128, s * 128 + Pn)), sl(None, fl), False, fc == NFC - 1)
                            hs = td.h(pl, s, SL(nh * 512, (nh + 1) * 512))
                            stt(hs, bks[s](pl), 0.5, hs, ALU.mult, ALU.add)

        def resid_proj(td, fam, srcT):
            S, Pn = td.S, td.Pn
            pl = SL(0, Pn)
            for nh in range(2):
                sl = load_chunk(fam, nh, [8, 512])
                for s in range(S):
                    bk_ = nextbank()
                    for kc in range(8):
                        mm(bk_(pl), srcT(None, kc, SL(s * 128, s * 128 + Pn)), sl(None, kc), kc == 0, kc == 7)
                    hs = td.h(pl, s, SL(nh * 512, (nh + 1) * 512))
                    tt("dve", hs, bk_(pl), hs, ALU.add)

        def fm_proj(td, sl, cols, evac):
            NT = td.NT
            for n4 in cols:
                bk_ = nextbank()
                for kc in range(8):
                    mm(bk_(None, SL(0, NT)), sl(None, kc, SL(n4 * 128, (n4 + 1) * 128)), xnT(None, kc, SL(0, NT)),
                       kc == 0, kc == 7)
                evac(n4, bk_(None, SL(0, NT)))

        def tm_proj(td, sl, s):
            Pn = td.Pn
            bk_ = nextbank()
            for kc in range(8):
                mm(bk_(SL(0, Pn)), xnT(None, kc, SL(s * 128, s * 128 + Pn)), sl(None, kc), kc == 0, kc == 7)
            return bk_

        def mix(td):
            NT, S, Pn = td.NT, td.S, td.Pn
            pl = SL(0, Pn)
            smp = td.kind == "sample"
            rmsnorm_T(td, 1, 8)
            pump("att")
            sl = load_chunk("win", 0, [8, 512])
            if smp:
                dma("pool", sp_stage(SL(0, 60)).ap, spool[:, :], [], [sp_stage()], "spool")
                for g in range(4):
                    bk_ = nextbank()
                    trp(bk_(None, SL(0, 60)), sp_stage(SL(0, 60), SL(g * 128, (g + 1) * 128)), identf(SL(0, 60), SL(0, 60)))
                    cpy("dve", xaT_s(None, g, None, SL(1, 16)),
                        Acc(bk_.ap[:, 0:60].rearrange("p (b t) -> p b t", b=4), bk_.buf, [(0, 2048)]))
                X = xaT_s
                nb_, Wd = 4, 32
                TA, TB, TC, PO = ta_s, tb_s, tc_s, pooled_s

                def ev_xa(n4, b_):
                    cpy("act", xaT_s(None, n4, None, SL(16, 32)),
                        Acc(b_.ap.rearrange("p (b t) -> p b t", b=4), b_.buf, b_.segs))
            else:
                X = xaT
                nb_, Wd = 1, 16 + TT
                TA, TB, TC, PO = ta, tb_, tc_, pooled

                def ev_xa(n4, b_):
                    cpy("act", xaT(None, n4, 0, SL(16, 16 + TT)), b_)
            fm_proj(td, sl, range(4), ev_xa)
            last_prompt = (td.kind == "prompt" and td.idx == N_PTILES - 1)
            if smp or last_prompt:
                s_ = td.S - 1
                bk_ = tm_proj(td, sl, s_)
                cpy("dve", ybuf(pl, 0, SL(0, D_POOL)), bk_(pl))
                if smp:
                    for b in range(4):
                        dma("act", pool_s[b], ybuf(SL(16 * b + 1, 16 * b + 16), 0, SL(0, D_POOL)).ap,
                            [ybuf(None, 0)], [], ("y", 0))
                else:
                    dma("act", pool_p[:, :], ybuf(SL(113, 128), 0, SL(0, D_POOL)).ap, [ybuf(None, 0)], [], ("y", 0))
            Wn = Wd - 16

            for g, w in enumerate(WINS):
                if g == 0:
                    tt("pool", PO(None, g), X(None, g, None, SL(16, Wd)), X(None, g, None, SL(15, Wd - 1)), ALU.add)
                else:
                    tt("pool", TA(None, None, SL(2, Wd)), X(None, g, None, SL(2, Wd)), X(None, g, None, SL(1, Wd - 1)), ALU.add)
                    if g == 1:
                        tt("pool", PO(None, g), TA(None, None, SL(16, Wd)), TA(None, None, SL(14, Wd - 2)), ALU.add)
                    else:
                        tt("pool", TB(None, None, SL(4, Wd)), TA(None, None, SL(4, Wd)), TA(None, None, SL(2, Wd - 2)), ALU.add)
                        if g == 2:
                            tt("pool", PO(None, g), TB(None, None, SL(16, Wd)), TB(None, None, SL(12, Wd - 4)), ALU.add)
                        else:
                            tt("pool", TC(None, None, SL(8, Wd)), TB(None, None, SL(8, Wd)), TB(None, None, SL(4, Wd - 4)), ALU.add)
                            tt("pool", PO(None, g), TC(None, None, SL(16, Wd)), TC(None, None, SL(8, Wd - 8)), ALU.add)
                if td.kind == "prompt" and td.idx == 0:
                    tt("pool", PO(None, g, 0, SL(0, 16)), PO(None, g, 0, SL(0, 16)), fixv(None, g), ALU.mult)
                dacc = dT(None, g, SL(0, NT))
                dacc = Acc(dacc.ap.rearrange("p (b t) -> p b t", b=nb_), dacc.buf, dacc.segs)
                stt(dacc, PO(None, g), 1.0 / w, X(None, g, None, SL(16, Wd)), ALU.mult, ALU.subtract)
            if td.kind == "prompt" and td.idx < N_PTILES - 1:
                cpy("pool", xaT(None, None, 0, SL(1, 16)), xaT(None, None, 0, SL(TT + 1, TT + 16)))
            sl = load_chunk("win", 1, [8, 512])
            fm_proj(td, sl, range(4), lambda n4, b_: act(uT(None, n4, SL(0, NT)), b_, AF.Gelu_apprx_tanh))
            sl = load_chunk("win", 2, [8, 512])
            for s in range(S):
                bk_ = tm_proj(td, sl, s)
                act(vtm(pl, s), bk_(pl), AF.Gelu_apprx_tanh, accum=vsum(pl, SL(s, s + 1)))
                act(Acc(junk_t[0:Pn, 0:D_SGU], "junk", [(0, 1024)]), vtm(pl, s), AF.Square, accum=vsq(pl, SL(s, s + 1)))
            sS = SL(0, S)
            ts("pool", lmean(pl, sS), vsum(pl, sS), 1.0 / D_SGU, None, ALU.mult)
            tt("pool", lmsq(pl, sS), lmean(pl, sS), lmean(pl, sS), ALU.mult)
            ts("pool", lvar(pl, sS), vsq(pl, sS), 1.0 / D_SGU, None, ALU.mult)
            tt("pool", lvar(pl, sS), lvar(pl, sS), lmsq(pl, sS), ALU.subtract)
            ts("pool", lvar(pl, sS), lvar(pl, sS), EPS, None, ALU.add)
            tt("pool", lrstd(pl, sS), lvar(pl, sS), neghalf(pl, sS), ALU.pow)
            for s in range(S):
                ts("dve", vtm(pl, s), vtm(pl, s), lmean(pl, SL(s, s + 1)), lrstd(pl, SL(s, s + 1)), ALU.subtract, ALU.mult)
                tt("dve", vnb(pl, s), vtm(pl, s), gsgu(pl), ALU.mult)
                if smp:
                    tt("pool", ybuf(pl, 1, SL(0, D_SGU)), vtm(pl, s), gsgu(pl), ALU.mult)
                    dma("act", sgu_v[:, :], ybuf(pl, 1, SL(0, D_SGU)).ap, [ybuf(None, 1)], [], ("y", 1))
            for g in range(4):
                bk_ = nextbank()
                mm(bk_(None, SL(0, NT)), wpool(None, g), dT(None, g, SL(0, NT)), True, True)
                act(aT(None, g, SL(0, NT)), bk_(None, SL(0, NT)), AF.Copy, scale=cfm(None, SL(56 + g, 57 + g)))

            def gates_half(nhf):
                for gi_, cj in ((0, 3 + nhf), (1, 5 + nhf)):
                    sl = load_chunk("win", cj, [8, 512])
                    base = gi_ * 8 + nhf * 4

                    def ev_gate(n4, b_, base=base):
                        act(gates(None, base + n4, SL(0, NT)), b_, AF.Sigmoid, bias=cfm(None, SL(40 + base + n4, 41 + base + n4)))
                    fm_proj(td, sl, range(4), ev_gate)

            def merged_half(nhf):
                sl = load_chunk("wpab", nhf, [2, 4, 512])
                for n4 in range(4):
                    n = nhf * 4 + n4
                    bA = nextbank()
                    bB = nextbank()
                    for kc in range(4):
                        mm(bA(None, SL(0, NT)), sl(None, 0, kc, SL(n4 * 128, (n4 + 1) * 128)), aT(None, kc, SL(0, NT)), kc == 0, kc == 3)
                    for kc in range(4):
                        mm(bB(None, SL(0, NT)), sl(None, 1, kc, SL(n4 * 128, (n4 + 1) * 128)), usT(None, kc, SL(0, NT)), kc == 0, kc == 3)
                    tt("dve", t1(None, n % 2, SL(0, NT)), bA(None, SL(0, NT)), gates(None, n, SL(0, NT)), ALU.mult)
                    tt("dve", t2(None, n % 2, SL(0, NT)), bB(None, SL(0, NT)), gates(None, 8 + n, SL(0, NT)), ALU.mult)
                    tt("dve", mergedT(None, n, SL(0, NT)), t1(None, n % 2, SL(0, NT)), t2(None, n % 2, SL(0, NT)), ALU.add)

            gates_half(0)
            for hh in range(4):
                bk_ = nextbank()
                if smp:
                    brow = Acc(bsrow_s.ap[0:1, hh].rearrange("o b t -> o (b t)"), "cstb", bsrow_s(SL(0, 1), hh).segs)
                else:
                    brow = Acc(bsrow.ap[0:1, hh].rearrange("o b t -> o (b t)"), "cstb", bsrow(SL(0, 1), hh).segs)
                mm(bk_(None, SL(0, NT)), ones(SL(0, 1)), brow, True, False)
                if smp:
                    mm(bk_(None, SL(0, NT)), vnb(pl, 0, SL(hh * 128, (hh + 1) * 128)), wblk(pl, hh), False, True)
                else:
                    for n in range(4):
                        mm(bk_(None, SL(n * 128, (n + 1) * 128)), vnb(None, n, SL(hh * 128, (hh + 1) * 128)), wmT(None, hh),
                           False, n == 3)
                tt("dve", usT(None, hh, SL(0, NT)), bk_(None, SL(0, NT)), uT(None, hh, SL(0, NT)), ALU.mult)
            merged_half(0)
            gates_half(1)
            merged_half(1)
            resid_proj(td, "wo", mergedT)

        def attn_stage1a(td, gi, KTg):
            R = td.R
            rl = SL(0, R)
            cols = SL(gi * R, (gi + 1) * R)
            k = gi % 4
            if td.kind == "prompt":
                sbk = [nextbank(F32, [2, 256], fixed=(gi % 2) * 2 + b_) for b_ in range(2)]
            else:
                sbk = [nextbank(F32, [2, 256]) for _ in range(2)]
            for hd in range(4):
                for dj in range(2):
                    mm(sbk[hd // 2](rl, hd % 2), qT(None, hd * 2 + dj, cols), KTg(None, hd * 2 + dj), dj == 0, dj == 1)
            for bi in range(2):
                o_ = mxv(rl, k, SL(bi * 2, bi * 2 + 2))
                i_ = sbk[bi](rl)
                tr.add("dve", (lambda o_=o_, i_=i_: nc.vector.tensor_reduce(out=o_.ap, in_=i_.ap, axis=AX.X, op=ALU.max)),
                       reads=[i_], writes=[o_])
            ts("dve", negb(rl, k), mxv(rl, k), -1.0 / 16.0, None, ALU.mult)
            return sbk

        def attn_stage1b(td, gi, sbk):
            R = td.R
            rl = SL(0, R)
            k = gi % 4
            kp = gi % 2
            for hd in range(4):
                act(Ptm(rl, kp, hd), sbk[hd // 2](rl, hd % 2), AF.Exp, scale=1.0 / 16.0, bias=negb(rl, k, SL(hd, hd + 1)),
                    accum=rsv(rl, k, SL(hd, hd + 1)))
            o2 = rinv(rl, k)
            i2 = rsv(rl, k)
            tr.add("dve", lambda: nc.vector.reciprocal(out=o2.ap, in_=i2.ap), reads=[i2], writes=[o2])
            tt("dve", Ptm(rl, kp), Ptm(rl, kp), rinv(rl, k), ALU.mult,
               in1_ap=rinv(rl, k).ap.unsqueeze(2).to_broadcast([R, 4, N_MEM]))

        def attn_stage1(td, gi, KTg):
            attn_stage1b(td, gi, attn_stage1a(td, gi, KTg))

        def attn_stage2(td, gi):
            R = td.R
            rl = SL(0, R)
            cols = SL(gi * R, (gi + 1) * R)
            k = gi % 2
            tbk = nextbank(BF16, [8, 128])
            for hd in range(4):
                for mc in range(2):
                    trp(tbk(None, hd * 2 + mc, SL(0, R)), Ptm(rl, k, hd, SL(mc * 128, (mc + 1) * 128)), ident(rl, SL(0, R)))
            cpy("act", PT(None, None, cols), tbk(None, None, SL(0, R)))

        def attn_group(td, gi, KTg, Vg):
            attn_stage1(td, gi, KTg)
            attn_stage2(td, gi)

        def smp_prep(gi):
            kk = gi % 2
            dma("pool", Kst[kk]().ap, ck[gi].rearrange("(mc p) d -> p mc d", p=128), [], [Kst[kk]()], ("kst", kk))
            dma("pool", Vs[kk]().ap, cv[gi].rearrange("(mc p) d -> p mc d", p=128), [], [Vs[kk]()], ("vs", kk))
            for mc in range(2):
                tbk = nextbank(BF16, [8, 128])
                for dc in range(8):
                    trp(tbk(None, dc), Kst[kk](None, mc, SL(dc * 128, (dc + 1) * 128)), ident())
                cpy("dve", KTs[kk](None, None, SL(mc * 128, (mc + 1) * 128)), tbk())

        def attention(td):
            NT = td.NT
            smp = td.kind == "sample"
            if smp:
                smp_prep(0)
            rmsnorm_T(td, 2, 16)
            pump("ffb", 12.0 / 28.0)
            for nh in range(2):
                sl = load_chunk("wq", nh, [8, 512])
                fm_proj(td, sl, range(4),
                        lambda n4, b_, nh=nh: cpy("act" if n4 % 2 == 0 else "dve", qT(None, nh * 4 + n4, SL(0, NT)), b_))
            ngroups = 4
            if smp:
                for gi in range(ngroups):
                    kk = gi % 2
                    if gi + 1 < ngroups:
                        smp_prep(gi + 1)
                    attn_group(td, gi, KTs[kk], Vs[kk])
                    R = td.R
                    cols = SL(gi * R, (gi + 1) * R)
                    obk = nextbank(F32, [8, 16])
                    for dcn in range(8):
                        hd = dcn // 2
                        for mc in range(2):
                            mm(obk(None, dcn), Vs[kk](None, mc, SL(dcn * 128, (dcn + 1) * 128)), PT(None, hd * 2 + mc, cols),
                               mc == 0, mc == 1)
                    cpy("dve", oT(None, None, cols), obk())
                resid_proj(td, "wco", oT)
            else:
                G = td.S
                sl_co = [load_chunk("wco", nh, [8, 512]) for nh in range(2)]

                def attn_tail(gi):
                    cols = SL(gi * 128, (gi + 1) * 128)
                    for half in range(2):
                        bk_ = nextbank(F32, [4, 128])
                        for q in range(4):
                            dcn = half * 4 + q
                            hd = dcn // 2
                            for mc in range(2):
                                mm(bk_(None, q), Vp(None, mc, SL(dcn * 128, (dcn + 1) * 128)), PT(None, hd * 2 + mc, cols),
                                   mc == 0, mc == 1)
                        cpy("act" if half == 0 else "dve", oT(None, SL(half * 4, half * 4 + 4), cols), bk_())
                    for nh in range(2):
                        bk_ = nextbank()
                        for kc in range(8):
                            mm(bk_(None), oT(None, kc, cols), sl_co[nh](None, kc), kc == 0, kc == 7)
                        hs = td.h(None, gi, SL(nh * 512, (nh + 1) * 512))
                        tt("dve", hs, bk_(None), hs, ALU.add)

                bank_pool[0] = [4, 5, 6, 7]
                sb_ = {0: attn_stage1a(td, 0, KT)}
                if G > 1:
                    sb_[1] = attn_stage1a(td, 1, KT)
                for gi in range(G):
                    attn_stage1b(td, gi, sb_.pop(gi))
                    if gi >= 1:
                        attn_stage2(td, gi - 1)
                        attn_tail(gi - 1)
                    if gi + 2 < G:
                        sb_[gi + 2] = attn_stage1a(td, gi + 2, KT)
                attn_stage2(td, G - 1)
                attn_tail(G - 1)
                bank_pool[0] = list(range(8))

        def final_norm(td, ydst):
            Pn = td.Pn
            pl = SL(0, Pn)
            for (a, b) in norm_groups(td):
                norm_stats(td, 4, a, b)
                for s in range(a, b):
                    stt(yfin(pl, s % 2), td.h(pl, s), rstd(pl, 4, SL(s, s + 1)), gfin(pl), ALU.mult, ALU.mult)
                    dma("act", ydst[s * 128:s * 128 + Pn, :], yfin(pl, s % 2).ap, [yfin(pl, s % 2)], [], ("yf", s % 2))

        def mem_kv(td, first_td):
            rmsnorm_T(td, 5, 32)
            rmsnorm_T(first_td, 0, 0, parts="A")
            pump("ffa")
            for nh in range(2):
                sl = load_chunk("wmk", nh, [8, 512])
                fm_proj(td, sl, range(4), lambda n4, b_, nh=nh: cpy("act", KT(None, nh * 4 + n4), b_))
                for s in range(2):
                    bk_ = tm_proj(td, sl, s)
                    cpy("dve", ybuf(None, s, SL(0, 512)), bk_())
                    dma("act", mk_o[s * 128:(s + 1) * 128, nh * 512:(nh + 1) * 512], ybuf(None, s, SL(0, 512)).ap,
                        [ybuf(None, s)], [], ("y", s))
            for nh in range(2):
                sl = load_chunk("wmv", nh, [8, 512])
                for s in range(2):
                    bk_ = tm_proj(td, sl, s)
                    cpy("act", Vp(None, s, SL(nh * 512, (nh + 1) * 512)), bk_())
                    cpy("dve", ybuf(None, s, SL(0, 512)), bk_())
                    dma("act", mv_o[s * 128:(s + 1) * 128, nh * 512:(nh + 1) * 512], ybuf(None, s, SL(0, 512)).ap,
                        [ybuf(None, s)], [], ("y", s))

        def run_tile(td, nxt, first, piggy=None):
            pl = SL(0, td.Pn)
            ydst = yp[td.idx * TT:(td.idx + 1) * TT, :] if td.kind == "prompt" else ys
            if first:
                rmsnorm_T(td, 0, 0, parts="B")
            if piggy is not None:
                plS = SL(0, piggy.Pn)
                dma("act", piggy.h(plS, 0).ap, xs[:, :], [], [piggy.h(None, 0)], ("xload", piggy.par))
                ffn(td, "a", 0, 0, prenormed=True, extra=piggy,
                    extra_prenorm=lambda: rmsnorm_T(piggy, 6, 0, dst=xnT_s))
            else:
                ffn(td, "a", 0, 0, prenormed=True)
            if nxt is not None:
                load_x(nxt)
            dump("h_ffn1_%s%d" % (td.kind[0], td.idx), td.h(pl, 0), [td.Pn, D])
            mix(td)
            dump("h_mix_%s%d" % (td.kind[0], td.idx), td.h(pl, 0), [td.Pn, D])
            attention(td)
            dump("h_att_%s%d" % (td.kind[0], td.idx), td.h(pl, 0), [td.Pn, D])
            if piggy is not None:
                dump("h_ffn1_s0", piggy.h(plS, 0), [piggy.Pn, D])
                mix(piggy)
                dump("h_mix_s0", piggy.h(plS, 0), [piggy.Pn, D])
                attention(piggy)
                dump("h_att_s0", piggy.h(plS, 0), [piggy.Pn, D])
            kw = {}
            if nxt is not None:
                kw = dict(hook_mid=lambda: rmsnorm_T(nxt, 0, 0, parts="A"),
                          hook_up_done=lambda: rmsnorm_T(nxt, 0, 0, parts="B"))
            if piggy is not None:
                kw.update(extra=piggy, extra_prenorm=lambda: rmsnorm_T(piggy, 7, 24, dst=xnT_s))
            ffn(td, "b", 3, 24, **kw)
            final_norm(td, ydst)
            if piggy is not None:
                final_norm(piggy, ys)

        emit_casts_upto(CAST_AHEAD - 1)
        mem_kv(mem_td, tiles[0])
        ptiles = [t for t in tiles if t.kind == "prompt"]
        stile = tiles[-1] if tiles[-1].kind == "sample" else None
        for i, t in enumerate(ptiles):
            lastp = (i == len(ptiles) - 1)
            run_tile(t, None if lastp else ptiles[i + 1], i == 0, piggy=(stile if lastp else None))

        tr.emit(nc, es)
    return nc, dump_outs


_CACHE = {}


def make_in_maps(inputs):
    f = lambda a: np.ascontiguousarray(np.asarray(a, dtype=np.float32))
    wts = {}
    for n in WEIGHT_NAMES:
        a = f(inputs[n])
        wts[n] = a if n == "g_final" else f(a[0])
    in_maps = []
    for c in range(NB):
        m = dict(wts)
        m["xp"] = f(inputs["x_prompt"][c])
        m["xs"] = f(inputs["x_sample"][4 * c:4 * c + 4]).reshape(SB_PER_CORE * DEC_T, D)
        m["spool"] = f(inputs["state_pool"][0, 4 * c:4 * c + 4]).reshape(SB_PER_CORE * 15, D_POOL)
        m["ck"] = f(inputs["cache_mem_k"][0, 4 * c:4 * c + 4]).reshape(SB_PER_CORE, N_MEM, D)
        m["cv"] = f(inputs["cache_mem_v"][0, 4 * c:4 * c + 4]).reshape(SB_PER_CORE, N_MEM, D)
        m["mem"] = f(inputs["mem_prompt"][c])
        in_maps.append(m)
    return in_maps


def kernel(**inputs):
    if "nc" not in _CACHE:
        _CACHE["nc"] = build_program()[0]
    nc = _CACHE["nc"]
    in_maps = make_in_maps(inputs)
    res = run_bass_kernel_spmd(nc, in_maps, core_ids=list(range(NB)))
    r = res.results
    g = lambda k: [np.asarray(r[c][k], dtype=np.float32) for c in range(NB)]
    y_prompt = np.stack(g("yp"), 0)
    y_sample = np.concatenate(g("ys"), 0).reshape(DEC_B, DEC_T, D)
    pool_prompt = np.stack(g("pool_p"), 0)[None]
    pool_sample = np.concatenate(g("pool_s"), 0)[None]
    sgu_v_sample = np.concatenate(g("sgu_v"), 0).reshape(DEC_B, DEC_T, D_SGU)[None]
    mem_k = np.stack(g("mk_o"), 0).reshape(NB, N_MEM, 4, 256)[None]
    mem_v = np.stack(g("mv_o"), 0).reshape(NB, N_MEM, 4, 256)[None]
    return (y_prompt, y_sample, pool_prompt, pool_sample, sgu_v_sample, mem_k, mem_v)
```

```python
import itertools
from contextlib import ExitStack

import numpy as np
import concourse.bass as bass
import concourse.mybir as mybir
from concourse.bass_utils import run_bass_kernel_spmd

F32 = mybir.dt.float32
BF16 = mybir.dt.bfloat16
AF = mybir.ActivationFunctionType
ALU = mybir.AluOpType
AX = mybir.AxisListType

ENGS = ("pe", "act", "dve", "pool", "sp")

D = 1024
SEQ = 4096
NB = 8
DEC_B = 32
DEC_T = 16
SB_PER_CORE = 4
N_MEM = 256
D_POOL = 512
D_SGU = 512
D_FF = 2816
NFC = 22
D_IN = 3584
EPS = 1e-6
NSLOT = 5
TT = 512
N_PTILES = SEQ // TT
WINS = (2, 4, 8, 16)


def _prod(xs):
    r = 1
    for x in xs:
        r *= x
    return r


class Acc:
    __slots__ = ("ap", "buf", "segs")

    def __init__(self, ap, buf, segs):
        self.ap = ap
        self.buf = buf
        self.segs = segs


class View:
    def __init__(self, ap, buf, off, shape, esz):
        self.ap = ap
        self.buf = buf
        self.off = off
        self.shape = tuple(shape)
        self.esz = esz

    def __call__(self, *idx):
        if len(idx) == 0:
            idx = (None,)
        p = idx[0] if idx[0] is not None else slice(None)
        fidx = [slice(None) if ix is None else ix for ix in idx[1:]]
        fidx = fidx + [slice(None)] * (len(self.shape) - len(fidx))
        rng = []
        for d, ix in enumerate(fidx):
            if isinstance(ix, int):
                assert 0 <= ix < self.shape[d], (self.buf, self.shape, idx)
                rng.append((ix, ix + 1))
            else:
                lo = 0 if ix.start is None else ix.start
                hi = self.shape[d] if ix.stop is None else ix.stop
                assert ix.step is None
                assert 0 <= lo < hi <= self.shape[d], (self.buf, self.shape, idx)
                rng.append((lo, hi))
        ap = self.ap[(p,) + tuple(fidx)]
        if isinstance(self.buf, tuple) and self.buf[0] == "psum":
            return Acc(ap, self.buf, [(0, 2048)])
        n = len(self.shape)
        strides = [_prod(self.shape[i + 1:]) for i in range(n)]
        k = n - 1
        while k > 0 and rng[k] == (0, self.shape[k]):
            k -= 1
        outer = [range(lo, hi) for (lo, hi) in rng[:k]]
        nruns = _prod([len(r) for r in outer]) if outer else 1
        segs = []
        if nruns > 64:
            lo = sum(rng[d][0] * strides[d] for d in range(n))
            hi = sum((rng[d][1] - 1) * strides[d] for d in range(n)) + 1
            segs.append((self.off + lo * self.esz, self.off + hi * self.esz))
        else:
            for combo in itertools.product(*outer):
                base = sum(c * strides[d] for d, c in enumerate(combo))
                lo = base + rng[k][0] * strides[k]
                hi = base + rng[k][1] * strides[k]
                segs.append((self.off + lo * self.esz, self.off + hi * self.esz))
        return Acc(ap, self.buf, segs)


class Op:
    __slots__ = ("eng", "fn", "deps", "signal", "cnt", "key", "dval", "pos", "dwaits")

    def __init__(self, eng, fn, key):
        self.pos = 0
        self.dwaits = {}
        self.eng = eng
        self.fn = fn
        self.deps = set()
        self.signal = False
        self.cnt = None
        self.key = key
        self.dval = None


class Tracker:
    def __init__(self):
        self.ops = {e: [] for e in ENGS}
        self.recs = {}
        self.dma_cnt = {}
        self.fence_deps = {e: None for e in ENGS}
        self.all_dma = []

    def _collect(self, op, acc, is_write):
        recs = self.recs.get(acc.buf)
        if not recs:
            return
        for (lo, hi) in acc.segs:
            for r in recs:
                if r[0] < hi and lo < r[1]:
                    d = r[3]
                    if d is op:
                        continue
                    rk = r[2]
                    if not is_write and rk == "R":
                        continue
                    if d.key is None and op.key is None and d.eng == op.eng:
                        if op.eng == "pe":
                            continue
                        if rk == "R" and op.eng != "pool":
                            continue
                    op.deps.add(d)

    def _record(self, op, acc, is_write):
        recs = self.recs.setdefault(acc.buf, [])
        for (lo, hi) in acc.segs:
            if is_write:
                recs[:] = [r for r in recs if not (lo <= r[0] and r[1] <= hi)]
                recs.append([lo, hi, "W", op])
            else:
                for r in recs:
                    if (r[2] == "R" and r[0] == lo and r[1] == hi and r[3].eng == op.eng
                            and (r[3].key is None) == (op.key is None)):
                        r[3] = op
                        break
                else:
                    recs.append([lo, hi, "R", op])

    def add(self, eng, fn, reads=(), writes=(), key=None, ndma=1):
        op = Op(eng, fn, key)
        fd = self.fence_deps[eng]
        if fd:
            op.deps.update(fd)
            self.fence_deps[eng] = None
        for a in reads:
            ex = isinstance(a.buf, tuple) and a.buf[0] == "psum"
            self._collect(op, a, ex)
        for a in writes:
            self._collect(op, a, True)
        for a in reads:
            ex = isinstance(a.buf, tuple) and a.buf[0] == "psum"
            self._record(op, a, ex)
        for a in writes:
            self._record(op, a, True)
        best = {}
        for d in op.deps:
            if d.key is None:
                b = best.get(d.eng)
                if b is None or d.pos > b.pos:
                    best[d.eng] = d
            else:
                op.dwaits[d.key] = self.dma_cnt[d.key]
        op.deps = set(best.values())
        if key is not None:
            self.dma_cnt[key] = self.dma_cnt.get(key, 0) + 16 * ndma
            op.dval = self.dma_cnt[key]
            self.all_dma.append(op)
        for d in op.deps:
            d.signal = True
        op.pos = len(self.ops[eng])
        self.ops[eng].append(op)
        return op

    def fence(self):
        deps = []
        for e in ENGS:
            comp = [o for o in self.ops[e] if o.key is None]
            if comp:
                comp[-1].signal = True
                deps.append(comp[-1])
        deps.extend(self.all_dma)
        for e in ENGS:
            self.fence_deps[e] = list(deps)

    def emit(self, nc, es):
        keys = sorted(self.dma_cnt.keys(), key=str)
        sem_eng = {e: es.enter_context(nc.semaphore("s_" + e)) for e in ENGS}
        sem_dma = {k: es.enter_context(nc.semaphore("d_%d" % i)) for i, k in enumerate(keys)}
        for e in ENGS:
            c = 0
            for op in self.ops[e]:
                if op.key is None and op.signal:
                    c += 1
                    op.cnt = c
        handles = {"pe": nc.tensor, "act": nc.scalar, "dve": nc.vector, "pool": nc.gpsimd, "sp": nc.sync}
        block = es.enter_context(nc.Block())
        tr = self

        def run(e):
            h = handles[e]
            waited = {}
            for op in tr.ops[e]:
                waits = {}
                for d in op.deps:
                    waits[("e", d.eng)] = d.cnt
                for dk, v in op.dwaits.items():
                    waits[("d", dk)] = v
                for k, v in waits.items():
                    if waited.get(k, 0) >= v:
                        continue
                    waited[k] = v
                    h.wait_ge(sem_dma[k[1]] if k[0] == "d" else sem_eng[k[1]], v)
                if op.key is not None:
                    op.fn(sem_dma[op.key])
                else:
                    ins = op.fn()
                    if op.signal:
                        ins.then_inc(sem_eng[e], 1)
            if e == "sp":
                for k in keys:
                    h.wait_ge(sem_dma[k], tr.dma_cnt[k])

        @block.tensor
        def _(t):
            run("pe")

        @block.scalar
        def _(t):
            run("act")

        @block.vector
        def _(t):
            run("dve")

        @block.gpsimd
        def _(t):
            run("pool")

        @block.sync
        def _(t):
            run("sp")


class TileD:
    def __init__(self, kind, idx, NT, S, Pn, R, h=None):
        self.h = h
        self.kind = kind
        self.idx = idx
        self.NT = NT
        self.S = S
        self.Pn = Pn
        self.R = R


WEIGHT_NAMES = ["g_ff1", "w1a", "w3a", "w2a", "g_mix", "w_in", "b_gate", "w_pool", "pool_scale", "g_sgu",
                "w_s", "b_s", "w_pa", "w_pb", "w_o", "g_mem", "w_mk", "w_mv", "g_ca", "w_q", "w_co",
                "g_ff2", "w1b", "w3b", "w2b", "g_final"]
WEIGHT_SHAPES = {
    "g_ff1": [D], "w1a": [D, D_FF], "w3a": [D, D_FF], "w2a": [D_FF, D], "g_mix": [D], "w_in": [D, D_IN],
    "b_gate": [2 * D], "w_pool": [4, 128, 128], "pool_scale": [D_POOL], "g_sgu": [D_SGU], "w_s": [4, 128, 128],
    "b_s": [4, 128], "w_pa": [D_POOL, D], "w_pb": [D_SGU, D], "w_o": [D, D], "g_mem": [D], "w_mk": [D, D],
    "w_mv": [D, D], "g_ca": [D], "w_q": [D, D], "w_co": [D, D], "g_ff2": [D], "w1b": [D, D_FF],
    "w3b": [D, D_FF], "w2b": [D_FF, D], "g_final": [D],
}


def build_program(n_ptiles=N_PTILES, do_sample=True, dumps=()):
    nc = bass.Bass("TRN2", target_bir_lowering=False)
    tr = Tracker()
    dumps = set(dumps)

    def din(name, shape):
        return nc.dram_tensor(name, shape, F32, kind="ExternalInput").ap()

    def dout(name, shape):
        return nc.dram_tensor(name, shape, F32, kind="ExternalOutput").ap()

    xp = din("xp", [SEQ, D])
    xs = din("xs", [SB_PER_CORE * DEC_T, D])
    spool = din("spool", [SB_PER_CORE * 15, D_POOL])
    ck = din("ck", [SB_PER_CORE, N_MEM, D])
    cv = din("cv", [SB_PER_CORE, N_MEM, D])
    mem = din("mem", [N_MEM, D])
    W = {n: din(n, WEIGHT_SHAPES[n]) for n in WEIGHT_NAMES}

    yp = dout("yp", [SEQ, D])
    ys = dout("ys", [SB_PER_CORE * DEC_T, D])
    pool_p = dout("pool_p", [15, D_POOL])
    pool_s = dout("pool_s", [SB_PER_CORE, 15, D_POOL])
    sgu_v = dout("sgu_v", [SB_PER_CORE * DEC_T, D_SGU])
    mk_o = dout("mk_o", [N_MEM, D])
    mv_o = dout("mv_o", [N_MEM, D])
    dump_outs = {}

    fams = {"ffu_a": 11, "ffd_a": 6, "win": 7, "wpab": 2, "wo": 2, "wq": 2, "wco": 2, "wmk": 2, "wmv": 2,
            "ffu_b": 11, "ffd_b": 6}
    scr = {f: nc.dram_tensor("scr_" + f, [n, 128, 4096], BF16, kind="Internal").ap() for f, n in fams.items()}

    with ExitStack() as es:
        E = es.enter_context

        def sb_alloc(name, shape, dt):
            esz = 4 if dt == F32 else 2
            t = E(nc.sbuf_tensor(name, [128, _prod(shape)], dt))
            ap = t[:, :]
            if len(shape) == 2:
                ap = ap.rearrange("p (a b) -> p a b", a=shape[0])
            elif len(shape) == 3:
                ap = ap.rearrange("p (a b c) -> p a b c", a=shape[0], b=shape[1])
            return View(ap, name, 0, shape, esz)

        hbufs = [sb_alloc("h0", [4, D], F32), sb_alloc("h1", [4, D], F32)]
        xnT = sb_alloc("xnT", [8, TT], BF16)
        xntm = sb_alloc("xntm", [4, D], BF16)
        KT = sb_alloc("KT", [8, N_MEM], BF16)
        Vp = sb_alloc("Vp", [2, D], BF16)
        xaT = sb_alloc("xaT", [4, 1, 16 + TT], F32)
        slots_t = [E(nc.sbuf_tensor("slot%d" % i, [128, 4096], BF16)) for i in range(NSLOT)]
        ybuf = sb_alloc("ybuf", [2, 512], F32)
        junk_t = E(nc.sbuf_tensor("junk", [128, D], BF16))
        sx_t = E(nc.sbuf_tensor("sx", [128, 2048], BF16))
        ARENA_B = 75968
        arena_t = E(nc.sbuf_tensor("arena", [128, ARENA_B // 4], F32))
        cst_t = E(nc.sbuf_tensor("cst", [128, 2448], F32))
        cstb_t = E(nc.sbuf_tensor("cstb", [128, 4352], BF16))
        st_t = E(nc.sbuf_tensor("st", [128, 208], F32))
        banks_t = [E(nc.psum_tensor("bank%d" % i, [128, 512], F32)) for i in range(8)]

        def carve(t, bufname, off_b, shape, dt):
            esz = 4 if dt == F32 else 2
            n = _prod(shape)
            tesz = 4 if t.dtype == F32 else 2
            assert off_b % 4 == 0
            ap = t[:, off_b // tesz:(off_b + ((n * esz + 3) // 4) * 4) // tesz]
            if dt != t.dtype:
                ap = ap.bitcast(dt)
            ap = ap[:, 0:n]
            if len(shape) == 2:
                ap = ap.rearrange("p (a b) -> p a b", a=shape[0])
            elif len(shape) == 3:
                ap = ap.rearrange("p (a b c) -> p a b c", a=shape[0], b=shape[1])
            return View(ap, bufname, off_b, shape, esz)

        class Carver:
            def __init__(self, t, name):
                self.t = t
                self.name = name
                self.off = 0

            def __call__(self, shape, dt, at=None):
                if at is not None:
                    self.off = at
                esz = 4 if dt == F32 else 2
                v = carve(self.t, self.name, self.off, shape, dt)
                self.off += ((_prod(shape) * esz + 31) // 32) * 32
                return v

        cc = Carver(cst_t, "cst")
        cfm = cc([64], F32)
        gfin = cc([D], F32)
        gsgu = cc([D_SGU], F32)
        fixv = cc([4, 16], F32)
        neghalf = cc([8], F32)
        identf = cc([128], F32)
        vec_tm = cc([128], F32)
        ws_f = cc([4, 128], F32)
        cb = Carver(cstb_t, "cstb")
        ident = cb([128], BF16)
        wmT = cb([4, 128], BF16)
        wblk = cb([4, 64], BF16)
        wpool = cb([4, 128], BF16)
        ones = cb([128], BF16)
        bsrow = cb([4, 4, 128], BF16)
        bsrow_s = cb([4, 4, 16], BF16)
        ws_b = cb([4, 128], BF16)
        sc = Carver(st_t, "st")
        ss = sc([8, 4], F32)
        tmpn = sc([8, 4], F32)
        rstd = sc([8, 4], F32)
        vsum = sc([4], F32)
        vsq = sc([4], F32)
        lmean = sc([4], F32)
        lmsq = sc([4], F32)
        lvar = sc([4], F32)
        lrstd = sc([4], F32)
        mxv = sc([4, 4], F32)
        negb = sc([4, 4], F32)
        rsv = sc([4, 4], F32)
        rinv = sc([4, 4], F32)

        sxc = Carver(sx_t, "sx")
        xnT_s = sxc([8, 64], BF16)
        h1T_s = sxc([NFC, 64], BF16)
        sil_s = sxc([2, 64], BF16)
        assert sxc.off <= 4096
        ar = Carver(arena_t, "arena")
        h1T = ar([NFC, TT], BF16, at=0)
        sil = ar([2, TT], BF16)
        yfin = ar([2, D], F32, at=14 * TT * 2)
        assert yfin.off + 8192 <= 22 * TT * 2
        uT = ar([4, TT], BF16, at=0)
        vtm = ar([4, D_SGU], F32)
        vnb = ar([4, D_SGU], BF16)
        gates = ar([16, TT], BF16)
        ta = ar([1, 16 + TT], F32)
        tb_ = ar([1, 16 + TT], F32)
        tc_ = ar([1, 16 + TT], F32)
        pooled = ar([4, 1, TT], F32)
        dT = ar([4, TT], BF16)
        aT = ar([4, TT], BF16)
        usT = ar([4, TT], BF16)
        mergedT = ar([8, TT], BF16)
        t1 = ar([2, TT], F32)
        t2 = ar([2, TT], F32)
        assert ar.off <= ARENA_B, ar.off
        qT = ar([8, TT], BF16, at=0)
        Ptm = ar([2, 4, N_MEM], BF16)
        PT = ar([8, TT], BF16)
        oT = ar([8, TT], BF16)
        attn_end = ar.off
        Kst = [ar([2, D], BF16, at=attn_end), ar([2, D], BF16)]
        KTs = [ar([8, N_MEM], BF16), ar([8, N_MEM], BF16)]
        Vs = [ar([2, D], BF16), ar([2, D], BF16)]
        assert ar.off <= ARENA_B, ar.off
        xaT_s = ar([4, 4, 32], F32, at=ta.off)
        ta_s = ar([4, 32], F32)
        tb_s = ar([4, 32], F32)
        tc_s = ar([4, 32], F32)
        pooled_s = ar([4, 4, 16], F32)
        sp_stage = ar([D_POOL], F32)
        assert ar.off <= dT.off, ar.off

        bank_rr = [0]
        bank_pool = [list(range(8))]

        def nextbank(dt=F32, shape=None, fixed=None):
            if fixed is not None:
                i = fixed
            else:
                i = bank_pool[0][bank_rr[0] % len(bank_pool[0])]
                bank_rr[0] += 1
            ap = banks_t[i][:, :]
            esz = 4
            if dt == BF16:
                ap = ap.bitcast(BF16)
                esz = 2
            shape = shape or [2048 // esz]
            ap = ap[:, 0:_prod(shape)]
            if len(shape) == 2:
                ap = ap.rearrange("p (a b) -> p a b", a=shape[0])
            return View(ap, ("psum", i), 0, shape, esz)

        def mm(out, lhsT, rhs, start, stop):
            tr.add("pe", lambda: nc.tensor.matmul(out.ap, lhsT=lhsT.ap, rhs=rhs.ap, start=start, stop=stop),
                   reads=[lhsT, rhs], writes=[out])

        def trp(out, in_, idn):
            tr.add("pe", lambda: nc.tensor.transpose(out.ap, in_.ap, idn.ap), reads=[in_, idn], writes=[out])

        def act(out, in_, func, scale=None, bias=None, accum=None, track_out=True):
            kw = {}
            rd = [in_]
            wr = [out] if track_out else []
            if scale is not None:
                if isinstance(scale, Acc):
                    kw["scale"] = scale.ap
                    rd.append(scale)
                else:
                    kw["scale"] = scale
            if bias is not None:
                if isinstance(bias, Acc):
                    kw["bias"] = bias.ap
                    rd.append(bias)
                else:
                    kw["bias"] = bias
            if accum is not None:
                kw["accum_out"] = accum.ap
                wr.append(accum)
            tr.add("act", lambda: nc.scalar.activation(out=out.ap, in_=in_.ap, func=func, **kw), reads=rd, writes=wr)

        def eh(eng):
            return nc.vector if eng == "dve" else nc.gpsimd

        def tt(eng, out, in0, in1, op, in1_ap=None):
            a1 = in1.ap if in1_ap is None else in1_ap
            tr.add(eng, lambda: eh(eng).tensor_tensor(out=out.ap, in0=in0.ap, in1=a1, op=op),
                   reads=[in0, in1], writes=[out])

        def ts(eng, out, in0, s1, s2, op0, op1=None):
            rd = [in0]
            a1 = s1
            a2 = s2
            if isinstance(s1, Acc):
                rd.append(s1)
                a1 = s1.ap
            if isinstance(s2, Acc):
                rd.append(s2)
                a2 = s2.ap
            if op1 is None:
                tr.add(eng, lambda: eh(eng).tensor_scalar(out=out.ap, in0=in0.ap, scalar1=a1, scalar2=None, op0=op0),
                       reads=rd, writes=[out])
            else:
                tr.add(eng, lambda: eh(eng).tensor_scalar(out=out.ap, in0=in0.ap, scalar1=a1, scalar2=a2, op0=op0, op1=op1),
                       reads=rd, writes=[out])

        def stt(out, in0, scalar, in1, op0, op1):
            rd = [in0, in1]
            a = scalar
            if isinstance(scalar, Acc):
                rd.append(scalar)
                a = scalar.ap
            tr.add("dve", lambda: nc.vector.scalar_tensor_tensor(out=out.ap, in0=in0.ap, scalar=a, in1=in1.ap, op0=op0, op1=op1),
                   reads=rd, writes=[out])

        def cpy(eng, out, in_):
            if eng == "act":
                tr.add("act", lambda: nc.scalar.copy(out=out.ap, in_=in_.ap), reads=[in_], writes=[out])
            else:
                tr.add(eng, lambda: eh(eng).tensor_copy(out=out.ap, in_=in_.ap), reads=[in_], writes=[out])

        def memset(eng, out, val):
            tr.add(eng, lambda: eh(eng).memset(out.ap, val), writes=[out])

        def dma(eng, out_ap, in_ap, reads, writes, key):
            q = {"sp": nc.sync, "pool": nc.gpsimd, "act": nc.scalar}[eng]
            tr.add(eng, lambda s: q.dma_start(out=out_ap, in_=in_ap).then_inc(s, 16), reads=reads, writes=writes, key=key)

        def dump(name, acc, shape):
            if name not in dumps:
                return
            o = dout("dbg_" + name, shape)
            dump_outs[name] = o
            dma("pool", o, acc.ap, [acc], [], ("dbg", name))

        def SL(a, b):
            return slice(a, b)

        def load_x(td):
            if td.kind == "prompt":
                t0 = td.idx * TT
                dma("sp", td.h().ap, xp[t0:t0 + TT, :].rearrange("(s p) d -> p s d", p=128), [], [td.h()], ("xload", td.par))
            elif td.kind == "sample":
                dma("sp", td.h(SL(0, td.Pn), 0).ap, xs[:, :], [], [td.h(None, 0)], ("xload", td.par))
            else:
                dma("sp", td.h(None, SL(0, 2)).ap, mem.rearrange("(s p) d -> p s d", p=128), [], [td.h(None, SL(0, 2))],
                    ("xload", td.par))

        tiles = [TileD("prompt", i, TT, 4, 128, 128) for i in range(n_ptiles)]
        if do_sample:
            tiles.append(TileD("sample", 0, SB_PER_CORE * DEC_T, 1, SB_PER_CORE * DEC_T, DEC_T))
        for i, t in enumerate(tiles):
            t.par = i % 2
            t.h = hbufs[t.par]
        mem_td = TileD("mem", 0, N_MEM, 2, 128, 128, h=hbufs[1])
        mem_td.par = 1
        load_x(mem_td)
        load_x(tiles[0])

        memset("pool", identf(), 0.0)
        tr.add("pool", lambda: nc.gpsimd.affine_select(out=identf().ap, in_=identf().ap, pattern=[[-1, 128]],
                                                       compare_op=ALU.not_equal, fill=1.0, base=0, channel_multiplier=1),
               reads=[identf()], writes=[identf()])
        cpy("dve", ident(), identf())
        memset("pool", neghalf(), -0.5)
        memset("pool", ones(), 1.0)
        memset("pool", xaT(None, SL(0, 4), 0, SL(0, 16)), 0.0)
        memset("pool", wblk(), 0.0)
        for g, w in enumerate(WINS):
            memset("pool", fixv(None, g), 1.0)
            for t in range(w - 1):
                memset("pool", fixv(None, g, SL(t, t + 1)), float(w) / float(t + 1))
        memset("pool", vec_tm(), 0.0)
        rows = [("g_ff1", 0, 8), ("g_mix", 8, 8), ("g_ca", 16, 8), ("g_ff2", 24, 8), ("g_mem", 32, 8),
                ("b_gate", 40, 16), ("pool_scale", 56, 4)]
        for ri, (nm, r0, nr) in enumerate(rows):
            wacc = Acc(None, "cst", [(vec_tm.off + 64 * ri, vec_tm.off + 64 * ri + 64)])
            dma("sp", vec_tm(SL(r0, r0 + nr)).ap, W[nm].rearrange("(r c) -> r c", c=128),
                [], [wacc], "cst_vec")
        dma("sp", gfin().ap, W["g_final"].partition_broadcast(128), [], [gfin()], "cst_g")
        dma("sp", gsgu().ap, W["g_sgu"].partition_broadcast(128), [], [gsgu()], "cst_g")
        dma("sp", ws_f().ap, W["w_s"].rearrange("h p q -> p h q"), [], [ws_f()], "cst_ws")
        for n in range(4):
            dma("pool", bsrow(SL(0, 1), None, n).ap, W["b_s"].rearrange("(o h) p -> o h p", o=1),
                [], [bsrow(SL(0, 1), None, n)], "cst_bs")
            dma("pool", bsrow_s(SL(0, 1), None, n).ap, W["b_s"].rearrange("(o h) p -> o h p", o=1)[:, :, 0:16],
                [], [bsrow_s(SL(0, 1), None, n)], "cst_bss")
        dma("pool", wpool().ap, W["w_pool"].rearrange("g c d -> c g d"), [], [wpool()], "cst_wp")
        bk = nextbank()
        trp(bk(None, SL(0, 64)), vec_tm(SL(0, 64)), identf(SL(0, 64), SL(0, 64)))
        cpy("dve", cfm(), bk(None, SL(0, 64)))
        cpy("dve", ws_b(), ws_f())
        bk = nextbank(BF16, [8, 128])
        for hh in range(4):
            trp(bk(None, hh), ws_b(None, hh), ident())
        cpy("dve", wmT(), bk(None, SL(0, 4)))
        memset("pool", wmT(SL(64, 128), None, SL(0, 64)), 0.0)
        for b in range(4):
            dma("sp", wblk(SL(16 * b, 16 * b + 16), None, SL(16 * b, 16 * b + 16)).ap,
                wmT(SL(0, 16), None, SL(0, 16)).ap, [wmT()], [wblk(None, None, SL(16 * b, 16 * b + 16))], "cst_wblk")


        chunk_parts = {}

        def conv(fam, j, parts):
            chunk_parts[(fam, j)] = parts

        def pump(phase, frac=1.0):
            return

        def kview(w, rows_per=128):
            return w.rearrange("(kc p) n -> p kc n", p=128)

        def conv_ffn(sfx):
            w1 = kview(W["w1" + sfx])
            w3 = kview(W["w3" + sfx])
            w2 = kview(W["w2" + sfx])
            for j in range(11):
                conv("ffu_" + sfx, j, [(0, 8, 256, w1[:, :, j * 256:(j + 1) * 256]),
                                       (2048, 8, 256, w3[:, :, j * 256:(j + 1) * 256])])
            for nh in range(2):
                for c in range(3):
                    f0 = 8 * c
                    nf = min(8, NFC - f0)
                    conv("ffd_" + sfx, nh * 3 + c, [(0, nf, 512, w2[:, f0:f0 + nf, nh * 512:(nh + 1) * 512])])

        def conv_sq(fam, w, js):
            wv = kview(w)
            for j in js:
                conv(fam, j, [(0, 8, 512, wv[:, :, j * 512:(j + 1) * 512])])

        conv_sq("wmk", W["w_mk"], range(2))
        conv_sq("wmv", W["w_mv"], range(2))
        conv_ffn("a")
        wpa = kview(W["w_pa"])
        wpb = kview(W["w_pb"])

        def conv_pab(nh):
            conv("wpab", nh, [(0, 4, 512, wpa[:, :, nh * 512:(nh + 1) * 512]),
                              (2048, 4, 512, wpb[:, :, nh * 512:(nh + 1) * 512])])
        conv_sq("win", W["w_in"], [0, 1, 2, 3, 5])
        conv_pab(0)
        conv_sq("win", W["w_in"], [4, 6])
        conv_pab(1)
        conv_sq("wo", W["w_o"], range(2))
        conv_sq("wq", W["w_q"], range(2))
        conv_sq("wco", W["w_co"], range(2))
        conv_ffn("b")

        slot_rr = [0]

        converted = set()
        first_order = list(chunk_parts.keys())
        cast_emitted = [0]
        CAST_AHEAD = NSLOT - 1

        def emit_casts_upto(q):
            q = min(q, len(first_order) - 1)
            while cast_emitted[0] <= q:
                c = cast_emitted[0]
                i = c % NSLOT
                for (c0, a, b, src) in chunk_parts[first_order[c]]:
                    dst = slots_t[i][:, c0:c0 + a * b].rearrange("p (a b) -> p a b", a=a)
                    dma("pool", dst, src, [], [Acc(None, "slot%d" % i, [(2 * c0, 2 * (c0 + a * b))])], ("slotc", i))
                cast_emitted[0] += 1

        def load_chunk(fam, j, shape):
            cnt = slot_rr[0]
            i = cnt % NSLOT
            slot_rr[0] += 1
            t = slots_t[i]
            n = _prod(shape)
            ap = t[:, 0:n]
            if len(shape) == 2:
                ap = ap.rearrange("p (a b) -> p a b", a=shape[0])
            elif len(shape) == 3:
                ap = ap.rearrange("p (a b c) -> p a b c", a=shape[0], b=shape[1])
            v = View(ap, "slot%d" % i, 0, shape, 2)
            if (fam, j) in converted:
                dma("sp", t[:, 0:n], scr[fam][j][:, 0:n], [Acc(None, ("scr", fam, j), [(0, 2)])], [v()], ("slot", i))
                return v
            assert first_order[cnt] == (fam, j), (first_order[cnt], fam, j)
            converted.add((fam, j))
            emit_casts_upto(cnt + (CAST_AHEAD - 1 if (fam == "wco" and j == 1) else CAST_AHEAD))
            if fam not in ("wmk", "wmv"):
                dma("sp", scr[fam][j][:, 0:n], t[:, 0:n], [v()], [Acc(None, ("scr", fam, j), [(0, 2)])], ("scw", i))
            return v

        def norm_groups(td):
            S = td.S
            return [(0, S)] if S <= 2 else [(0, 2), (2, S)]

        def norm_stats(td, nidx, a, b):
            Pn = td.Pn
            pl = SL(0, Pn)
            for s in range(a, b):
                act(Acc(junk_t[0:Pn, :], "junk", [(0, 2048)]), td.h(pl, s), AF.Square, accum=ss(pl, nidx, SL(s, s + 1)))
            ts("pool", tmpn(pl, nidx, SL(a, b)), ss(pl, nidx, SL(a, b)), 1.0 / D, EPS, ALU.mult, ALU.add)
            tt("pool", rstd(pl, nidx, SL(a, b)), tmpn(pl, nidx, SL(a, b)), neghalf(pl, SL(a, b)), ALU.pow)

        def rmsnorm_T(td, nidx, gcol, parts="AB", dst=None):
            dst = xnT if dst is None else dst
            Pn = td.Pn
            pl = SL(0, Pn)
            groups = norm_groups(td)

            def copy_scaled(s, eng):
                if eng == "act":
                    act(xntm(pl, s), td.h(pl, s), AF.Copy, scale=rstd(pl, nidx, SL(s, s + 1)))
                else:
                    ts("dve", xntm(pl, s), td.h(pl, s), rstd(pl, nidx, SL(s, s + 1)), None, ALU.mult)

            def tr_evac(s):
                bv = nextbank(BF16, [8, 128])
                for kc in range(8):
                    trp(bv(None, kc, SL(0, Pn)), xntm(pl, s, SL(kc * 128, (kc + 1) * 128)), ident(pl, SL(0, Pn)))
                g = cfm(None, SL(gcol, gcol + 8))
                tt("dve", dst(None, None, SL(s * 128, s * 128 + Pn)), bv(None, None, SL(0, Pn)), g, ALU.mult,
                   in1_ap=g.ap.unsqueeze(2).to_broadcast([128, 8, Pn]))

            if "A" in parts:
                for (a, b) in groups:
                    norm_stats(td, nidx, a, b)
            last = len(groups) - 1
            for gi_, (a, b) in enumerate(groups):
                if "A" in parts:
                    for s in range(a, b):
                        copy_scaled(s, "act" if (gi_ == last and s == a) else "dve")
                if "B" in parts:
                    for s in range(a, b):
                        tr_evac(s)

        def ffn(td, sfx, nidx, gcol, prenormed=False, hook_mid=None, hook_up_done=None, extra=None, extra_prenorm=None):
            NT, S, Pn = td.NT, td.S, td.Pn
            pl = SL(0, Pn)
            if not prenormed:
                rmsnorm_T(td, nidx, gcol)
            if sfx == "a":
                pump("mix")
            else:
                pump("ffb")
            for j in range(11):
                sl = load_chunk("ffu_" + sfx, j, [2, 8, 256])
                for half in range(2):
                    fc = 2 * j + half
                    bA = nextbank()
                    bB = nextbank()
                    for (bk_, wi) in ((bA, 0), (bB, 1)):
                        for kc in range(8):
                            mm(bk_(None, SL(0, NT)), sl(None, wi, kc, SL(half * 128, (half + 1) * 128)),
                               xnT(None, kc, SL(0, NT)), kc == 0, kc == 7)
                    act(sil(None, fc % 2, SL(0, NT)), bA(None, SL(0, NT)), AF.Silu)
                    tt("dve", h1T(None, fc, SL(0, NT)), bB(None, SL(0, NT)), sil(None, fc % 2, SL(0, NT)), ALU.mult)
                if extra is not None:
                    if j == 0 and extra_prenorm is not None:
                        extra_prenorm()
                    for half in range(2):
                        fc = 2 * j + half
                        bS = nextbank()
                        for wi in range(2):
                            for kc in range(8):
                                mm(bS(None, SL(wi * 64, wi * 64 + 64)), sl(None, wi, kc, SL(half * 128, (half + 1) * 128)),
                                   xnT_s(None, kc), kc == 0, kc == 7)
                        act(sil_s(None, fc % 2), bS(None, SL(0, 64)), AF.Silu)
                        tt("dve", h1T_s(None, fc), bS(None, SL(64, 128)), sil_s(None, fc % 2), ALU.mult)
                if j == 5 and hook_mid is not None:
                    hook_mid()
            if hook_up_done is not None:
                hook_up_done()
            for nh in range(2):
                bks = [nextbank() for _ in range(S)]
                if extra is not None:
                    bkS = nextbank()
                    plS = SL(0, extra.Pn)
                for c in range(3):
                    f0 = 8 * c
                    nf = min(8, NFC - f0)
                    sl = load_chunk("ffd_" + sfx, nh * 3 + c, [nf, 512])
                    if extra is not None:
                        for fl in range(nf):
                            fc = f0 + fl
                            mm(bkS(plS), h1T_s(None, fc), sl(None, fl), fc == 0, fc == NFC - 1)
                        if c == 2:
                            hsS = extra.h(plS, 0, SL(nh * 512, (nh + 1) * 512))
                            stt(hsS, bkS(plS), 0.5, hsS, ALU.mult, ALU.add)
                    if c < 2:
                        for fl in range(nf):
                            fc = f0 + fl
                            for s in range(S):
                                mm(bks[s](pl), h1T(None, fc, SL(s * 128, s * 128 + Pn)), sl(None, fl), fc == 0, False)
                    else:
                        for s in range(S):
                            for fl in range(nf):
                                fc = f0 + fl
                                mm(bks[s](pl), h1T(None, fc, SL(s * 128, s * 128 + Pn)), sl(None, fl), False, fc == NFC - 1)
                            hs = td.h(pl, s, SL(nh * 512, (nh + 1) * 512))
                            stt(hs, bks[s](pl), 0.5, hs, ALU.mult, ALU.add)

        def resid_proj(td, fam, srcT):
            S, Pn = td.S, td.Pn
            pl = SL(0, Pn)
            for nh in range(2):
                sl = load_chunk(fam, nh, [8, 512])
                for s in range(S):
                    bk_ = nextbank()
                    for kc in range(8):
                        mm(bk_(pl), srcT(None, kc, SL(s * 128, s * 128 + Pn)), sl(None, kc), kc == 0, kc == 7)
                    hs = td.h(pl, s, SL(nh * 512, (nh + 1) * 512))
                    tt("dve", hs, bk_(pl), hs, ALU.add)

        def fm_proj(td, sl, cols, evac):
            NT = td.NT
            for n4 in cols:
                bk_ = nextbank()
                for kc in range(8):
                    mm(bk_(None, SL(0, NT)), sl(None, kc, SL(n4 * 128, (n4 + 1) * 128)), xnT(None, kc, SL(0, NT)),
                       kc == 0, kc == 7)
                evac(n4, bk_(None, SL(0, NT)))

        def tm_proj(td, sl, s):
            Pn = td.Pn
            bk_ = nextbank()
            for kc in range(8):
                mm(bk_(SL(0, Pn)), xnT(None, kc, SL(s * 128, s * 128 + Pn)), sl(None, kc), kc == 0, kc == 7)
            return bk_

        def mix(td):
            NT, S, Pn = td.NT, td.S, td.Pn
            pl = SL(0, Pn)
            smp = td.kind == "sample"
            rmsnorm_T(td, 1, 8)
            pump("att")
            sl = load_chunk("win", 0, [8, 512])
            if smp:
                dma("pool", sp_stage(SL(0, 60)).ap, spool[:, :], [], [sp_stage()], "spool")
                for g in range(4):
                    bk_ = nextbank()
                    trp(bk_(None, SL(0, 60)), sp_stage(SL(0, 60), SL(g * 128, (g + 1) * 128)), identf(SL(0, 60), SL(0, 60)))
                    cpy("dve", xaT_s(None, g, None, SL(1, 16)),
                        Acc(bk_.ap[:, 0:60].rearrange("p (b t) -> p b t", b=4), bk_.buf, [(0, 2048)]))
                X = xaT_s
                nb_, Wd = 4, 32
                TA, TB, TC, PO = ta_s, tb_s, tc_s, pooled_s

                def ev_xa(n4, b_):
                    cpy("act", xaT_s(None, n4, None, SL(16, 32)),
                        Acc(b_.ap.rearrange("p (b t) -> p b t", b=4), b_.buf, b_.segs))
            else:
                X = xaT
                nb_, Wd = 1, 16 + TT
                TA, TB, TC, PO = ta, tb_, tc_, pooled

                def ev_xa(n4, b_):
                    cpy("act", xaT(None, n4, 0, SL(16, 16 + TT)), b_)
            fm_proj(td, sl, range(4), ev_xa)
            last_prompt = (td.kind == "prompt" and td.idx == N_PTILES - 1)
            if smp or last_prompt:
                s_ = td.S - 1
                bk_ = tm_proj(td, sl, s_)
                cpy("dve", ybuf(pl, 0, SL(0, D_POOL)), bk_(pl))
                if smp:
                    for b in range(4):
                        dma("act", pool_s[b], ybuf(SL(16 * b + 1, 16 * b + 16), 0, SL(0, D_POOL)).ap,
                            [ybuf(None, 0)], [], ("y", 0))
                else:
                    dma("act", pool_p[:, :], ybuf(SL(113, 128), 0, SL(0, D_POOL)).ap, [ybuf(None, 0)], [], ("y", 0))
            Wn = Wd - 16

            for g, w in enumerate(WINS):
                if g == 0:
                    tt("pool", PO(None, g), X(None, g, None, SL(16, Wd)), X(None, g, None, SL(15, Wd - 1)), ALU.add)
                else:
                    tt("pool", TA(None, None, SL(2, Wd)), X(None, g, None, SL(2, Wd)), X(None, g, None, SL(1, Wd - 1)), ALU.add)
                    if g == 1:
                        tt("pool", PO(None, g), TA(None, None, SL(16, Wd)), TA(None, None, SL(14, Wd - 2)), ALU.add)
                    else:
                        tt("pool", TB(None, None, SL(4, Wd)), TA(None, None, SL(4, Wd)), TA(None, None, SL(2, Wd - 2)), ALU.add)
                        if g == 2:
                            tt("pool", PO(None, g), TB(None, None, SL(16, Wd)), TB(None, None, SL(12, Wd - 4)), ALU.add)
                        else:
                            tt("pool", TC(None, None, SL(8, Wd)), TB(None, None, SL(8, Wd)), TB(None, None, SL(4, Wd - 4)), ALU.add)
                            tt("pool", PO(None, g), TC(None, None, SL(16, Wd)), TC(None, None, SL(8, Wd - 8)), ALU.add)
                if td.kind == "prompt" and td.idx == 0:
                    tt("pool", PO(None, g, 0, SL(0, 16)), PO(None, g, 0, SL(0, 16)), fixv(None, g), ALU.mult)
                dacc = dT(None, g, SL(0, NT))
                dacc = Acc(dacc.ap.rearrange("p (b t) -> p b t", b=nb_), dacc.buf, dacc.segs)
                stt(dacc, PO(None, g), 1.0 / w, X(None, g, None, SL(16, Wd)), ALU.mult, ALU.subtract)
            if td.kind == "prompt" and td.idx < N_PTILES - 1:
                cpy("pool", xaT(None, None, 0, SL(1, 16)), xaT(None, None, 0, SL(TT + 1, TT + 16)))
            sl = load_chunk("win", 1, [8, 512])
            fm_proj(td, sl, range(4), lambda n4, b_: act(uT(None, n4, SL(0, NT)), b_, AF.Gelu_apprx_tanh))
            sl = load_chunk("win", 2, [8, 512])
            for s in range(S):
                bk_ = tm_proj(td, sl, s)
                act(vtm(pl, s), bk_(pl), AF.Gelu_apprx_tanh, accum=vsum(pl, SL(s, s + 1)))
                act(Acc(junk_t[0:Pn, 0:D_SGU], "junk", [(0, 1024)]), vtm(pl, s), AF.Square, accum=vsq(pl, SL(s, s + 1)))
            sS = SL(0, S)
            ts("pool", lmean(pl, sS), vsum(pl, sS), 1.0 / D_SGU, None, ALU.mult)
            tt("pool", lmsq(pl, sS), lmean(pl, sS), lmean(pl, sS), ALU.mult)
            ts("pool", lvar(pl, sS), vsq(pl, sS), 1.0 / D_SGU, None, ALU.mult)
            tt("pool", lvar(pl, sS), lvar(pl, sS), lmsq(pl, sS), ALU.subtract)
            ts("pool", lvar(pl, sS), lvar(pl, sS), EPS, None, ALU.add)
            tt("pool", lrstd(pl, sS), lvar(pl, sS), neghalf(pl, sS), ALU.pow)
            for s in range(S):
                ts("dve", vtm(pl, s), vtm(pl, s), lmean(pl, SL(s, s + 1)), lrstd(pl, SL(s, s + 1)), ALU.subtract, ALU.mult)
                tt("dve", vnb(pl, s), vtm(pl, s), gsgu(pl), ALU.mult)
                if smp:
                    tt("pool", ybuf(pl, 1, SL(0, D_SGU)), vtm(pl, s), gsgu(pl), ALU.mult)
                    dma("act", sgu_v[:, :], ybuf(pl, 1, SL(0, D_SGU)).ap, [ybuf(None, 1)], [], ("y", 1))
            for g in range(4):
                bk_ = nextbank()
                mm(bk_(None, SL(0, NT)), wpool(None, g), dT(None, g, SL(0, NT)), True, True)
                act(aT(None, g, SL(0, NT)), bk_(None, SL(0, NT)), AF.Copy, scale=cfm(None, SL(56 + g, 57 + g)))

            def gates_half(nhf):
                for gi_, cj in ((0, 3 + nhf), (1, 5 + nhf)):
                    sl = load_chunk("win", cj, [8, 512])
                    base = gi_ * 8 + nhf * 4

                    def ev_gate(n4, b_, base=base):
                        act(gates(None, base + n4, SL(0, NT)), b_, AF.Sigmoid, bias=cfm(None, SL(40 + base + n4, 41 + base + n4)))
                    fm_proj(td, sl, range(4), ev_gate)

            def merged_half(nhf):
                sl = load_chunk("wpab", nhf, [2, 4, 512])
                for n4 in range(4):
                    n = nhf * 4 + n4
                    bA = nextbank()
                    bB = nextbank()
                    for kc in range(4):
                        mm(bA(None, SL(0, NT)), sl(None, 0, kc, SL(n4 * 128, (n4 + 1) * 128)), aT(None, kc, SL(0, NT)), kc == 0, kc == 3)
                    for kc in range(4):
                        mm(bB(None, SL(0, NT)), sl(None, 1, kc, SL(n4 * 128, (n4 + 1) * 128)), usT(None, kc, SL(0, NT)), kc == 0, kc == 3)
                    tt("dve", t1(None, n % 2, SL(0, NT)), bA(None, SL(0, NT)), gates(None, n, SL(0, NT)), ALU.mult)
                    tt("dve", t2(None, n % 2, SL(0, NT)), bB(None, SL(0, NT)), gates(None, 8 + n, SL(0, NT)), ALU.mult)
                    tt("dve", mergedT(None, n, SL(0, NT)), t1(None, n % 2, SL(0, NT)), t2(None, n % 2, SL(0, NT)), ALU.add)

            gates_half(0)
            for hh in range(4):
                bk_ = nextbank()
                if smp:
                    brow = Acc(bsrow_s.ap[0:1, hh].rearrange("o b t -> o (b t)"), "cstb", bsrow_s(SL(0, 1), hh).segs)
                else:
                    brow = Acc(bsrow.ap[0:1, hh].rearrange("o b t -> o (b t)"), "cstb", bsrow(SL(0, 1), hh).segs)
                mm(bk_(None, SL(0, NT)), ones(SL(0, 1)), brow, True, False)
                if smp:
                    mm(bk_(None, SL(0, NT)), vnb(pl, 0, SL(hh * 128, (hh + 1) * 128)), wblk(pl, hh), False, True)
                else:
                    for n in range(4):
                        mm(bk_(None, SL(n * 128, (n + 1) * 128)), vnb(None, n, SL(hh * 128, (hh + 1) * 128)), wmT(None, hh),
                           False, n == 3)
                tt("dve", usT(None, hh, SL(0, NT)), bk_(None, SL(0, NT)), uT(None, hh, SL(0, NT)), ALU.mult)
            merged_half(0)
            gates_half(1)
            merged_half(1)
            resid_proj(td, "wo", mergedT)

        def attn_stage1a(td, gi, KTg):
            R = td.R
            rl = SL(0, R)
            cols = SL(gi * R, (gi + 1) * R)
            k = gi % 4
            if td.kind == "prompt":
                sbk = [nextbank(F32, [2, 256], fixed=(gi % 2) * 2 + b_) for b_ in range(2)]
            else:
                sbk = [nextbank(F32, [2, 256]) for _ in range(2)]
            for hd in range(4):
                for dj in range(2):
                    mm(sbk[hd // 2](rl, hd % 2), qT(None, hd * 2 + dj, cols), KTg(None, hd * 2 + dj), dj == 0, dj == 1)
            for bi in range(2):
                o_ = mxv(rl, k, SL(bi * 2, bi * 2 + 2))
                i_ = sbk[bi](rl)
                tr.add("dve", (lambda o_=o_, i_=i_: nc.vector.tensor_reduce(out=o_.ap, in_=i_.ap, axis=AX.X, op=ALU.max)),
                       reads=[i_], writes=[o_])
            ts("dve", negb(rl, k), mxv(rl, k), -1.0 / 16.0, None, ALU.mult)
            return sbk

        def attn_stage1b(td, gi, sbk):
            R = td.R
            rl = SL(0, R)
            k = gi % 4
            kp = gi % 2
            for hd in range(4):
                act(Ptm(rl, kp, hd), sbk[hd // 2](rl, hd % 2), AF.Exp, scale=1.0 / 16.0, bias=negb(rl, k, SL(hd, hd + 1)),
                    accum=rsv(rl, k, SL(hd, hd + 1)))
            o2 = rinv(rl, k)
            i2 = rsv(rl, k)
            tr.add("dve", lambda: nc.vector.reciprocal(out=o2.ap, in_=i2.ap), reads=[i2], writes=[o2])
            tt("dve", Ptm(rl, kp), Ptm(rl, kp), rinv(rl, k), ALU.mult,
               in1_ap=rinv(rl, k).ap.unsqueeze(2).to_broadcast([R, 4, N_MEM]))

        def attn_stage1(td, gi, KTg):
            attn_stage1b(td, gi, attn_stage1a(td, gi, KTg))

        def attn_stage2(td, gi):
            R = td.R
            rl = SL(0, R)
            cols = SL(gi * R, (gi + 1) * R)
            k = gi % 2
            tbk = nextbank(BF16, [8, 128])
            for hd in range(4):
                for mc in range(2):
                    trp(tbk(None, hd * 2 + mc, SL(0, R)), Ptm(rl, k, hd, SL(mc * 128, (mc + 1) * 128)), ident(rl, SL(0, R)))
            cpy("act", PT(None, None, cols), tbk(None, None, SL(0, R)))

        def attn_group(td, gi, KTg, Vg):
            attn_stage1(td, gi, KTg)
            attn_stage2(td, gi)

        def smp_prep(gi):
            kk = gi % 2
            dma("pool", Kst[kk]().ap, ck[gi].rearrange("(mc p) d -> p mc d", p=128), [], [Kst[kk]()], ("kst", kk))
            dma("pool", Vs[kk]().ap, cv[gi].rearrange("(mc p) d -> p mc d", p=128), [], [Vs[kk]()], ("vs", kk))
            for mc in range(2):
                tbk = nextbank(BF16, [8, 128])
                for dc in range(8):
                    trp(tbk(None, dc), Kst[kk](None, mc, SL(dc * 128, (dc + 1) * 128)), ident())
                cpy("dve", KTs[kk](None, None, SL(mc * 128, (mc + 1) * 128)), tbk())

        def attention(td):
            NT = td.NT
            smp = td.kind == "sample"
            if smp:
                smp_prep(0)
            rmsnorm_T(td, 2, 16)
            pump("ffb", 12.0 / 28.0)
            for nh in range(2):
                sl = load_chunk("wq", nh, [8, 512])
                fm_proj(td, sl, range(4), lambda n4, b_, nh=nh: cpy("act", qT(None, nh * 4 + n4, SL(0, NT)), b_))
            ngroups = 4
            if smp:
                for gi in range(ngroups):
                    kk = gi % 2
                    if gi + 1 < ngroups:
                        smp_prep(gi + 1)
                    attn_group(td, gi, KTs[kk], Vs[kk])
                    R = td.R
                    cols = SL(gi * R, (gi + 1) * R)
                    obk = nextbank(F32, [8, 16])
                    for dcn in range(8):
                        hd = dcn // 2
                        for mc in range(2):
                            mm(obk(None, dcn), Vs[kk](None, mc, SL(dcn * 128, (dcn + 1) * 128)), PT(None, hd * 2 + mc, cols),
                               mc == 0, mc == 1)
                    cpy("dve", oT(None, None, cols), obk())
                resid_proj(td, "wco", oT)
            else:
                G = td.S
                sl_co = [load_chunk("wco", nh, [8, 512]) for nh in range(2)]

                def attn_tail(gi):
                    cols = SL(gi * 128, (gi + 1) * 128)
                    for half in range(2):
                        bk_ = nextbank(F32, [4, 128])
                        for q in range(4):
                            dcn = half * 4 + q
                            hd = dcn // 2
                            for mc in range(2):
                                mm(bk_(None, q), Vp(None, mc, SL(dcn * 128, (dcn + 1) * 128)), PT(None, hd * 2 + mc, cols),
                                   mc == 0, mc == 1)
                        cpy("act" if half == 0 else "dve", oT(None, SL(half * 4, half * 4 + 4), cols), bk_())
                    for nh in range(2):
                        bk_ = nextbank()
                        for kc in range(8):
                            mm(bk_(None), oT(None, kc, cols), sl_co[nh](None, kc), kc == 0, kc == 7)
                        hs = td.h(None, gi, SL(nh * 512, (nh + 1) * 512))
                        tt("dve", hs, bk_(None), hs, ALU.add)

                bank_pool[0] = [4, 5, 6, 7]
                sb_ = {0: attn_stage1a(td, 0, KT)}
                if G > 1:
                    sb_[1] = attn_stage1a(td, 1, KT)
                for gi in range(G):
                    attn_stage1b(td, gi, sb_.pop(gi))
                    if gi >= 1:
                        attn_stage2(td, gi - 1)
                        attn_tail(gi - 1)
                    if gi + 2 < G:
                        sb_[gi + 2] = attn_stage1a(td, gi + 2, KT)
                attn_stage2(td, G - 1)
                attn_tail(G - 1)
                bank_pool[0] = list(range(8))

        def final_norm(td, ydst):
            Pn = td.Pn
            pl = SL(0, Pn)
            for (a, b) in norm_groups(td):
                norm_stats(td, 4, a, b)
                for s in range(a, b):
                    stt(yfin(pl, s % 2), td.h(pl, s), rstd(pl, 4, SL(s, s + 1)), gfin(pl), ALU.mult, ALU.mult)
                    dma("act", ydst[s * 128:s * 128 + Pn, :], yfin(pl, s % 2).ap, [yfin(pl, s % 2)], [], ("yf", s % 2))

        def mem_kv(td, first_td):
            rmsnorm_T(td, 5, 32)
            rmsnorm_T(first_td, 0, 0, parts="A")
            pump("ffa")
            for nh in range(2):
                sl = load_chunk("wmk", nh, [8, 512])
                fm_proj(td, sl, range(4), lambda n4, b_, nh=nh: cpy("act", KT(None, nh * 4 + n4), b_))
                for s in range(2):
                    bk_ = tm_proj(td, sl, s)
                    cpy("dve", ybuf(None, s, SL(0, 512)), bk_())
                    dma("act", mk_o[s * 128:(s + 1) * 128, nh * 512:(nh + 1) * 512], ybuf(None, s, SL(0, 512)).ap,
                        [ybuf(None, s)], [], ("y", s))
            for nh in range(2):
                sl = load_chunk("wmv", nh, [8, 512])
                for s in range(2):
                    bk_ = tm_proj(td, sl, s)
                    cpy("act", Vp(None, s, SL(nh * 512, (nh + 1) * 512)), bk_())
                    cpy("dve", ybuf(None, s, SL(0, 512)), bk_())
                    dma("act", mv_o[s * 128:(s + 1) * 128, nh * 512:(nh + 1) * 512], ybuf(None, s, SL(0, 512)).ap,
                        [ybuf(None, s)], [], ("y", s))

        def run_tile(td, nxt, first, piggy=None):
            pl = SL(0, td.Pn)
            ydst = yp[td.idx * TT:(td.idx + 1) * TT, :] if td.kind == "prompt" else ys
            if first:
                rmsnorm_T(td, 0, 0, parts="B")
            if piggy is not None:
                plS = SL(0, piggy.Pn)
                dma("act", piggy.h(plS, 0).ap, xs[:, :], [], [piggy.h(None, 0)], ("xload", piggy.par))
                ffn(td, "a", 0, 0, prenormed=True, extra=piggy,
                    extra_prenorm=lambda: rmsnorm_T(piggy, 6, 0, dst=xnT_s))
            else:
                ffn(td, "a", 0, 0, prenormed=True)
            if nxt is not None:
                load_x(nxt)
            dump("h_ffn1_%s%d" % (td.kind[0], td.idx), td.h(pl, 0), [td.Pn, D])
            mix(td)
            dump("h_mix_%s%d" % (td.kind[0], td.idx), td.h(pl, 0), [td.Pn, D])
            attention(td)
            dump("h_att_%s%d" % (td.kind[0], td.idx), td.h(pl, 0), [td.Pn, D])
            if piggy is not None:
                dump("h_ffn1_s0", piggy.h(plS, 0), [piggy.Pn, D])
                mix(piggy)
                dump("h_mix_s0", piggy.h(plS, 0), [piggy.Pn, D])
                attention(piggy)
                dump("h_att_s0", piggy.h(plS, 0), [piggy.Pn, D])
            kw = {}
            if nxt is not None:
                kw = dict(hook_mid=lambda: rmsnorm_T(nxt, 0, 0, parts="A"),
                          hook_up_done=lambda: rmsnorm_T(nxt, 0, 0, parts="B"))
            if piggy is not None:
                kw.update(extra=piggy, extra_prenorm=lambda: rmsnorm_T(piggy, 7, 24, dst=xnT_s))
            ffn(td, "b", 3, 24, **kw)
            final_norm(td, ydst)
            if piggy is not None:
                final_norm(piggy, ys)

        emit_casts_upto(CAST_AHEAD - 1)
        mem_kv(mem_td, tiles[0])
        ptiles = [t for t in tiles if t.kind == "prompt"]
        stile = tiles[-1] if tiles[-1].kind == "sample" else None
        for i, t in enumerate(ptiles):
            lastp = (i == len(ptiles) - 1)
            run_tile(t, None if lastp else ptiles[i + 1], i == 0, piggy=(stile if lastp else None))

        tr.emit(nc, es)
    return nc, dump_outs


_CACHE = {}


def make_in_maps(inputs):
    f = lambda a: np.ascontiguousarray(np.asarray(a, dtype=np.float32))
    wts = {}
    for n in WEIGHT_NAMES:
        a = f(inputs[n])
        wts[n] = a if n == "g_final" else f(a[0])
    in_maps = []
    for c in range(NB):
        m = dict(wts)
        m["xp"] = f(inputs["x_prompt"][c])
        m["xs"] = f(inputs["x_sample"][4 * c:4 * c + 4]).reshape(SB_PER_CORE * DEC_T, D)
        m["spool"] = f(inputs["state_pool"][0, 4 * c:4 * c + 4]).reshape(SB_PER_CORE * 15, D_POOL)
        m["ck"] = f(inputs["cache_mem_k"][0, 4 * c:4 * c + 4]).reshape(SB_PER_CORE, N_MEM, D)
        m["cv"] = f(inputs["cache_mem_v"][0, 4 * c:4 * c + 4]).reshape(SB_PER_CORE, N_MEM, D)
        m["mem"] = f(inputs["mem_prompt"][c])
        in_maps.append(m)
    return in_maps


def kernel(**inputs):
    if "nc" not in _CACHE:
        _CACHE["nc"] = build_program()[0]
    nc = _CACHE["nc"]
    in_maps = make_in_maps(inputs)
    res = run_bass_kernel_spmd(nc, in_maps, core_ids=list(range(NB)))
    r = res.results
    g = lambda k: [np.asarray(r[c][k], dtype=np.float32) for c in range(NB)]
    y_prompt = np.stack(g("yp"), 0)
    y_sample = np.concatenate(g("ys"), 0).reshape(DEC_B, DEC_T, D)
    pool_prompt = np.stack(g("pool_p"), 0)[None]
    pool_sample = np.concatenate(g("pool_s"), 0)[None]
    sgu_v_sample = np.concatenate(g("sgu_v"), 0).reshape(DEC_B, DEC_T, D_SGU)[None]
    mem_k = np.stack(g("mk_o"), 0).reshape(NB, N_MEM, 4, 256)[None]
    mem_v = np.stack(g("mv_o"), 0).reshape(NB, N_MEM, 4, 256)[None]
    return (y_prompt, y_sample, pool_prompt, pool_sample, sgu_v_sample, mem_k, mem_v)
```
